# Optimizing a Trainium2 kernel written in Bass

```python
import math
import jax, jax.numpy as jnp
from jax import lax
import numpy as np

D_MODEL = 1024
BATCH = 1
SEQ = 16384
DEPTH = 4

CHUNK = 64
D_MIX = D_MODEL
D_LRU = D_MIX // 2
LRU_BLOCKS = 8
LRU_BLOCK = D_LRU // LRU_BLOCKS
CONV_WIDTH = 4
LRU_C = 8.0
GLA_HEADS = 4
GLA_DV = (D_MIX - D_LRU) // GLA_HEADS
GLA_DK = GLA_DV // 2
GLA_K = GLA_HEADS * GLA_DK
GLA_V = GLA_HEADS * GLA_DV
GATE_RANK = 16
GATE_TAU = 16.0
D_FF = ((8 * D_MODEL // 3 + 255) // 256) * 256
EPS = 1e-6

SPLIT_SIZES = (D_LRU, D_LRU, GLA_K, GLA_K, GLA_V, GLA_V, GATE_RANK)
P_IN = sum(SPLIT_SIZES)
SPLIT_IDX = tuple(int(v) for v in np.cumsum(SPLIT_SIZES)[:-1])

kernel_name = "hymba_style_rglru_gla_hybrid"


def rms_norm(x, gain):
    xf = x.astype(jnp.float32)
    y = xf * lax.rsqrt(jnp.mean(xf * xf, axis=-1, keepdims=True) + EPS)
    return (y * gain.astype(jnp.float32)).astype(x.dtype)


def causal_depthwise_conv(x, w, b):
    seq = x.shape[1]
    xp = jnp.pad(x, ((0, 0), (CONV_WIDTH - 1, 0), (0, 0)))
    y = xp[:, 0:seq, :] * w[0]
    for j in range(1, CONV_WIDTH):
        y = y + xp[:, j:j + seq, :] * w[j]
    return y + b


def block_diag_linear(x, w, b):
    xb = x.reshape(x.shape[:-1] + (LRU_BLOCKS, LRU_BLOCK))
    y = jnp.einsum("bsni,nij->bsnj", xb, w.astype(x.dtype))
    return y.reshape(x.shape) + b.astype(x.dtype)


def rg_lru(x, w_a, b_a, w_i, b_i, lam):
    xf = x.astype(jnp.float32)
    r = jax.nn.sigmoid(block_diag_linear(xf, w_a, b_a))
    i = jax.nn.sigmoid(block_diag_linear(xf, w_i, b_i))
    log_a = -LRU_C * r * jax.nn.softplus(-lam.astype(jnp.float32))
    a = jnp.exp(log_a)
    u = jnp.sqrt(-jnp.expm1(2.0 * log_a)) * (i * xf)

    def step(h, au):
        a_t, u_t = au
        h = a_t * h + u_t
        return h, h

    h0 = jnp.zeros((xf.shape[0], xf.shape[2]), jnp.float32)
    _, hs = lax.scan(step, h0, (a.swapaxes(0, 1), u.swapaxes(0, 1)))
    return hs.swapaxes(0, 1).astype(x.dtype)


def gla_chunk_causal(q, k, v, log_alpha):
    bsz, seq = q.shape[:2]
    nc = seq // CHUNK

    def rs(t):
        return t.astype(jnp.float32).reshape(bsz, nc, CHUNK, GLA_HEADS, t.shape[-1])

    q, k, v, la = rs(q), rs(k), rs(v), rs(log_alpha)
    bcum = jnp.cumsum(la, axis=2)
    b_end = bcum[:, :, -1:]
    k_dec = k * jnp.exp(b_end - bcum)
    q_dec = q * jnp.exp(b_end)
    scores = jnp.einsum("bnshd,bnthd->bnhst", q, k_dec)
    o_intra = jnp.einsum("bnhst,bnthv->bnshv", scores, v)
    updates = jnp.einsum("bnthd,bnthv->bnhdv", k_dec, v)
    decay = jnp.exp(b_end[:, :, 0])

    def step(state, inp):
        d, upd = inp
        return d[..., None] * state + upd, state

    s0 = jnp.zeros((bsz, GLA_HEADS, GLA_DK, GLA_DV), jnp.float32)
    _, s_prev = lax.scan(step, s0, (decay.swapaxes(0, 1), updates.swapaxes(0, 1)))
    s_prev = s_prev.swapaxes(0, 1)
    o_inter = jnp.einsum("bnshd,bnhdv->bnshv", q_dec, s_prev)
    return (o_intra + o_inter).reshape(bsz, seq, GLA_HEADS, GLA_DV)


def setup_inputs(seed: int = 0) -> dict:
    key = jax.random.key(seed)
    ks = jax.random.split(key, 24)
    f32 = jnp.float32

    def nrm(k, shape, scale):
        return jax.random.normal(k, shape, f32) * scale

    def gain(k, shape):
        return 1.0 + 0.02 * jax.random.normal(k, shape, f32)

    u = jax.random.uniform(ks[10], (DEPTH, D_LRU), f32, 0.9, 0.999)
    s = u ** (1.0 / LRU_C)
    lam = jnp.log(s) - jnp.log1p(-s)
    return {
        "x": jax.random.normal(ks[0], (BATCH, SEQ, D_MODEL), f32),
        "norm1": gain(ks[1], (DEPTH, D_MODEL)),
        "w_in": nrm(ks[2], (DEPTH, D_MODEL, P_IN), D_MODEL ** -0.5),
        "conv_w": nrm(ks[3], (DEPTH, CONV_WIDTH, D_LRU), CONV_WIDTH ** -0.5),
        "conv_b": nrm(ks[4], (DEPTH, D_LRU), 0.02),
        "lru_wa": nrm(ks[5], (DEPTH, LRU_BLOCKS, LRU_BLOCK, LRU_BLOCK), LRU_BLOCK ** -0.5),
        "lru_ba": nrm(ks[6], (DEPTH, D_LRU), 0.02),
        "lru_wi": nrm(ks[7], (DEPTH, LRU_BLOCKS, LRU_BLOCK, LRU_BLOCK), LRU_BLOCK ** -0.5),
        "lru_bi": nrm(ks[8], (DEPTH, D_LRU), 0.02),
        "lru_lambda": lam,
        "gla_w_alpha": nrm(ks[11], (DEPTH, GATE_RANK, GLA_K), GATE_RANK ** -0.5),
        "gla_b_alpha": nrm(ks[12], (DEPTH, GLA_K), 0.02),
        "gla_norm": gain(ks[13], (DEPTH, GLA_DV)),
        "w_out": nrm(ks[14], (DEPTH, D_MIX, D_MODEL), D_MIX ** -0.5),
        "norm2": gain(ks[15], (DEPTH, D_MODEL)),
        "w_ffn_in": nrm(ks[16], (DEPTH, D_MODEL, 2 * D_FF), D_MODEL ** -0.5),
        "w_ffn_out": nrm(ks[17], (DEPTH, D_FF, D_MODEL), D_FF ** -0.5),
        "final_norm": gain(ks[18], (D_MODEL,)),
    }


def reference(x, norm1, w_in, conv_w, conv_b, lru_wa, lru_ba, lru_wi, lru_bi,
              lru_lambda, gla_w_alpha, gla_b_alpha, gla_norm, w_out, norm2,
              w_ffn_in, w_ffn_out, final_norm):
    bsz, seq, _ = x.shape
    q_scale = GLA_DK ** -0.5
    for l in range(DEPTH):
        h = rms_norm(x, norm1[l])
        p = h @ w_in[l]
        lru_x, lru_g, q, k, v, g, z = jnp.split(p, SPLIT_IDX, axis=-1)

        lru_x = causal_depthwise_conv(lru_x, conv_w[l], conv_b[l])
        lru_o = rg_lru(lru_x, lru_wa[l], lru_ba[l], lru_wi[l], lru_bi[l],
                       lru_lambda[l]) * jax.nn.gelu(lru_g)

        zf = z.astype(jnp.float32) @ gla_w_alpha[l].astype(jnp.float32) + gla_b_alpha[l]
        log_alpha = jax.nn.log_sigmoid(zf) / GATE_TAU
        qh = (q * q_scale).reshape(bsz, seq, GLA_HEADS, GLA_DK)
        kh = k.reshape(bsz, seq, GLA_HEADS, GLA_DK)
        vh = v.reshape(bsz, seq, GLA_HEADS, GLA_DV)
        lah = log_alpha.reshape(bsz, seq, GLA_HEADS, GLA_DK)
        o = gla_chunk_causal(qh, kh, vh, lah)
        o = rms_norm(o, gla_norm[l]).reshape(bsz, seq, GLA_V).astype(x.dtype)
        gla_o = o * jax.nn.silu(g)

        mix = jnp.concatenate([lru_o, gla_o], axis=-1) @ w_out[l]
        x = x + mix

        h = rms_norm(x, norm2[l])
        gate, up = jnp.split(h @ w_ffn_in[l], 2, axis=-1)
        x = x + (jax.nn.silu(gate) * up) @ w_ffn_out[l]
    return rms_norm(x, final_norm)
```

```python
import numpy as np
from contextlib import ExitStack
import concourse.bass as bass
import concourse.mybir as mybir
from concourse.bass_utils import run_bass_kernel_spmd

F32 = mybir.dt.float32
BF16 = mybir.dt.bfloat16
ALU = mybir.AluOpType
AF = mybir.ActivationFunctionType

NCORES = 8
D = 1024
KC = 8
TT = 512
PIN = 2576
DFF = 2816
EPS = 1e-6
PV_G1, PV_G2, PV_CW, PV_CB, PV_BA, PV_BI, PV_LAM, PV_GN, NPV = 0, 8, 16, 32, 36, 40, 44, 48, 49
XW = 266
SLOT_ELEMS = 2048
NSLOT = 4


class Buf:
    __slots__ = ("name", "w", "r", "psum")

    def __init__(self, name, psum=False):
        self.name = name
        self.w = None
        self.r = []
        self.psum = psum


class FW:
    ENG = ("pe", "act", "dve", "pool", "sp")

    def __init__(self, nc, es):
        self.nc = nc
        self.es = es
        self.sem = {e: es.enter_context(nc.semaphore("sem_" + e)) for e in self.ENG}
        self.cnt = {e: 0 for e in self.ENG}
        self.prog = {e: [] for e in self.ENG}
        self.seen = {e: {} for e in self.ENG}

    def new_sem(self, name):
        return self.es.enter_context(self.nc.semaphore(name))

    def _collect(self, eng, reads, writes, extra):
        deps = []
        for b in reads:
            if b.w is not None:
                deps.append(b.w)
            if b.psum:
                deps.extend(b.r)
        for b in writes:
            if b.w is not None:
                deps.append(b.w)
            deps.extend(b.r)
        deps.extend(extra)
        waits = {}
        mysem = self.sem[eng]
        for (s, v) in deps:
            if eng == "pe" and s is mysem:
                continue
            if self.seen[eng].get(id(s), 0) >= v:
                continue
            if waits.get(id(s), (None, 0))[1] < v:
                waits[id(s)] = (s, v)
        for (s, v) in waits.values():
            self.seen[eng][id(s)] = v
        return list(waits.values())

    def op(self, eng, fn, reads=(), writes=(), extra=(), signal=True):
        waits = self._collect(eng, reads, writes, extra)
        sem = self.sem[eng]
        if signal:
            self.cnt[eng] += 1
            ev = (sem, self.cnt[eng])
        else:
            ev = (sem, self.cnt[eng] + 1)
        self.prog[eng].append(("op", fn, waits, self.cnt[eng] if signal else None, None, None))
        for b in reads:
            b.r.append(ev)
        for b in writes:
            b.w = ev
            b.r = []
        return ev

    def dma(self, eng, fn, dsem, dval, reads=(), writes=(), extra=(), inc=16):
        waits = self._collect(eng, reads, writes, extra)
        ev = (dsem, dval)
        self.prog[eng].append(("dma", fn, waits, None, dsem, inc))
        for b in reads:
            b.r.append(ev)
        for b in writes:
            b.w = ev
            b.r = []
        return ev

    def wait_only(self, eng, events):
        self.prog[eng].append(("wait", None, list(events), None, None, None))

    def finish(self):
        engsem = {id(self.sem[e]): e for e in self.ENG}
        waited = {e: set() for e in self.ENG}
        for e in self.ENG:
            for (_k, _f, waits, _c, _d, _i) in self.prog[e]:
                for (s, v) in waits:
                    if id(s) in engsem:
                        waited[engsem[id(s)]].add(v)
        remap = {e: {v: i + 1 for i, v in enumerate(sorted(waited[e]))} for e in self.ENG}
        self.n_signals = {e: len(remap[e]) for e in self.ENG}

        def run(eng, h):
            sem = self.sem[eng]
            for (kind, fn, waits, c, dsem, inc) in self.prog[eng]:
                for (s, v) in waits:
                    if id(s) in engsem:
                        v = remap[engsem[id(s)]][v]
                    h.wait_ge(s, v)
                if kind == "wait":
                    continue
                ins = fn(h)
                if kind == "op":
                    if c is not None and c in remap[eng]:
                        ins.then_inc(sem, 1)
                else:
                    if inc is None:
                        ins.then_inc(dsem)
                    else:
                        ins.then_inc(dsem, inc)

        with self.nc.Block() as block:
            @block.tensor
            def _(h):
                run("pe", h)

            @block.scalar
            def _(h):
                run("act", h)

            @block.vector
            def _(h):
                run("dve", h)

            @block.gpsimd
            def _(h):
                run("pool", h)

            @block.sync
            def _(h):
                run("sp", h)


def build(NT=2048, NL=4):
    NTILE = NT // TT
    NCH = NT // 64
    nc = bass.Bass("TRN2", target_bir_lowering=False)
    di = lambda name, shape: nc.dram_tensor(name, shape, F32, kind="ExternalInput").ap()
    xT = di("xT", [128, KC, NT])
    w_in = di("w_in", [NL, D, PIN])
    w_out = di("w_out", [NL, D, D])
    w_fi = di("w_ffn_in", [NL, D, 2 * DFF])
    w_fo = di("w_ffn_out", [NL, DFF, D])
    pvec_d = di("pvec", [128, NL * NPV])
    fnorm_d = di("fnorm", [128, KC])
    bd_d = di("bd", [128, NL, 8, 128])
    waug_d = di("waug", [17, NL, 256])
    U_d = di("Umat", [128, 128])
    E_d = di("Emat", [128, 2])
    masks_d = di("masks", [128, 24])
    yT = nc.dram_tensor("yT", [128, KC, NT], F32, kind="ExternalOutput").ap()
    ag_in = [nc.dram_tensor(f"agin{l}", [128, XW], F32) for l in range(NL)]
    ag_out = [nc.dram_tensor(f"agout{l}", [NCORES * 128, XW], F32) for l in range(NL)]
    tl_in = [nc.dram_tensor(f"tlin{l}", [128, 12], F32) for l in range(NL)]
    tl_out = [nc.dram_tensor(f"tlout{l}", [NCORES * 128, 12], F32) for l in range(NL)]

    with ExitStack() as es:
        fw = FW(nc, es)

        def sb(name, shape, dt=F32):
            return es.enter_context(nc.sbuf_tensor(name, shape, dt))

        xres = sb("xres", [128, KC, NT]); Bx = [[Buf(f"x{k}_{j}") for j in range(NTILE)] for k in range(KC)]
        hg = sb("hg", [128, 4, NT], BF16); Bhg = [[Buf("hg") for j in range(NTILE)] for c in range(4)]
        Ag = sb("Ag", [128, 4, NT], BF16); BAg = [[Buf("Ag") for j in range(NTILE)] for c in range(4)]
        Sloc = sb("Sloc", [128, 2, NCH, 128], BF16); BSloc = [[Buf("Sloc") for c in range(NCH)] for f in range(2)]
        qb = sb("qb", [128, 2, NT], BF16); Bqb = [[Buf("qb") for j in range(NTILE)] for f in range(2)]
        sg = sb("sg", [128, 4, NT], BF16); Bsg = [[Buf("sg") for j in range(NTILE)] for c in range(4)]
        wsl = sb("wsl", [128, NSLOT, SLOT_ELEMS], BF16); Bw = [Buf(f"w{i}") for i in range(NSLOT)]
        hb = sb("hb", [128, KC, TT], BF16); Bhb = [Buf(f"hb{k}") for k in range(KC)]
        hid = sb("hid", [128, 8, TT], BF16); Bhid = [Buf(f"hid{k}") for k in range(8)]
        xpre = sb("xpre", [128, 4, TT + 3]); Bxpre = [Buf(f"xpre{c}") for c in range(4)]
        NTMP = 9
        tmp = sb("tmp", [128, NTMP, TT]); Bt = [Buf(f"t{i}") for i in range(NTMP)]
        zaug = tmp[0:32, 7, :]; Bzaug = Bt[7]
        spt = tmp[:, 0, 0:256]; Bsp = Bt[0]
        dect = tmp[:, 1, 0:256]; Bdec = Bt[1]
        kdec = sb("kdec", [128, 256], BF16); Bkdec = Buf("kdec")
        Sst = sb("Sst", [128, 2, 256]); BS = [Buf("S0"), Buf("S1")]
        dcy = sb("dcy", [128, 2, NCH]); Bdcy = Buf("dcy")
        Dcum = sb("Dcum", [128, 2, NCH]); BDcum = Buf("Dcum")
        pvec = sb("pvec_s", [128, NL * NPV]); Bpv = Buf("pvec")
        fnorm = sb("fnorm_s", [128, KC])
        bdb = sb("bdb", [128, 8, 128], BF16); Bbd = Buf("bdb")
        waug = sb("waug_s", [17, 256]); Bwaug = Buf("waug")
        Um = sb("Um", [128, 128]); Em = sb("Em", [128, 2]); masks = sb("masks_s", [128, 24]); Bconst = Buf("const")
        ones_b = sb("ones_b", [128, 128], BF16); Bones = Buf("ones")
        epsc = sb("epsc", [128, 1]); zcol = sb("zcol", [128, 1])
        clam = sb("clam", [128, 8]); Bclam = Buf("clam")
        carry = sb("carry", [128, 8]); Bcarry = Buf("carry")
        xch = sb("xch", [128, XW]); Bxch = Buf("xch")
        tails = sb("tails", [128, 12]); Btails = Buf("tails")
        tgath = sb("tgath", [128, NCORES, 12]); Btg = Buf("tgath")
        init = sb("init", [128, 260]); Binit = Buf("init")
        Sinb = sb("Sinb", [128, 2, 128], BF16); BSinb = Buf("Sinb")
        cmb = sb("cmb", [128, 272]); Bcmb = Buf("cmb")
        banks = [es.enter_context(nc.psum_tensor(f"ps{i}", [128, 512], F32)) for i in range(8)]
        Bbank = [Buf(f"bank{i}", psum=True) for i in range(8)]
        Bg0a = Bbank[4]
        rot_state = {"i": 0, "lst": [0, 1, 2, 3]}

        def rot():
            lst = rot_state["lst"]
            i = lst[rot_state["i"] % len(lst)]
            rot_state["i"] += 1
            return banks[i], Bbank[i]
        G0, G1, G2, GS = banks[4], banks[5], banks[6], banks[7]
        Bg1, Bg2, Bgs = Bbank[5], Bbank[6], Bbank[7]

        s_x = fw.new_sem("s_x"); s_c = fw.new_sem("s_c"); s_out = fw.new_sem("s_out")
        s_w = [fw.new_sem(f"s_w{i}") for i in range(NSLOT)]
        s_misc = fw.new_sem("s_misc"); s_bd = fw.new_sem("s_bd"); s_wa = fw.new_sem("s_wa")
        cnt = {"c": 0, "misc": 0, "w": [0] * NSLOT}

        def dma_c(eng, out, in_, reads=(), writes=()):
            cnt["c"] += 1
            return fw.dma(eng, lambda h: h.dma_start(out=out, in_=in_), s_c, 16 * cnt["c"], reads=reads, writes=writes)

        def dma_m(eng, out, in_, reads=(), writes=()):
            cnt["misc"] += 1
            return fw.dma(eng, lambda h: h.dma_start(out=out, in_=in_), s_misc, 16 * cnt["misc"], reads=reads, writes=writes)

        def act(out, in_, func, reads, writes, bias=None, scale=None):
            kw = {}
            if bias is not None:
                kw["bias"] = bias
            if scale is not None:
                kw["scale"] = scale
            return fw.op("act", lambda h: h.activation(out=out, in_=in_, func=func, **kw), reads=reads, writes=writes)

        def tt(out, in0, in1, op, reads, writes, eng="dve"):
            return fw.op(eng, lambda h: h.tensor_tensor(out=out, in0=in0, in1=in1, op=op), reads=reads, writes=writes)

        def ts(out, in0, s1, s2, op0, op1, reads, writes, eng="dve"):
            if s2 is None:
                return fw.op(eng, lambda h: h.tensor_scalar(out=out, in0=in0, scalar1=s1, scalar2=None, op0=op0), reads=reads, writes=writes)
            return fw.op(eng, lambda h: h.tensor_scalar(out=out, in0=in0, scalar1=s1, scalar2=s2, op0=op0, op1=op1), reads=reads, writes=writes)

        def stt(out, in0, scalar, in1, op0, op1, reads, writes, eng="dve"):
            return fw.op(eng, lambda h: h.scalar_tensor_tensor(out=out, in0=in0, scalar=scalar, in1=in1, op0=op0, op1=op1), reads=reads, writes=writes)

        def mm(out, lhsT, rhs, start, stop, reads, writes, signal):
            return fw.op("pe", lambda h: h.matmul(out, lhsT=lhsT, rhs=rhs, start=start, stop=stop), reads=reads, writes=writes, signal=signal)

        wq = []
        wstate = {"issued": 0, "released": 0, "used": 0}

        def wview(w3, l, c0, ncols, k0=0, nk=KC):
            return w3[l, k0 * 128:(k0 + nk) * 128, c0:c0 + ncols].rearrange("(k p) m -> p k m", p=128)

        def w_issue():
            while wstate["issued"] < len(wq) and wstate["issued"] < wstate["released"] + NSLOT:
                j = wstate["issued"]
                ap, nk, ncols = wq[j]
                slot = j % NSLOT
                dst = wsl[:, slot, 0:nk * ncols].rearrange("p (k m) -> p k m", m=ncols)
                half = max(1, nk // 2)
                k = 0
                while k < nk:
                    kk = min(half, nk - k)
                    cnt["w"][slot] += 1
                    fw.dma("pool", lambda h, d=dst[:, k:k + kk, :], s=ap[:, k:k + kk, :]: h.dma_start(out=d, in_=s),
                           s_w[slot], 16 * cnt["w"][slot], writes=[Bw[slot]])
                    k += kk
                wstate["issued"] += 1

        def wnext():
            i = wstate["used"]
            wstate["used"] += 1
            w_issue()
            assert i < wstate["issued"], "weight block not issued (too many held)"
            ap, nk, ncols = wq[i]
            slot = i % NSLOT
            return wsl[:, slot, 0:nk * ncols].rearrange("p (k m) -> p k m", m=ncols), Bw[slot]

        def wrel(n=1):
            wstate["released"] += n
            assert wstate["released"] <= wstate["used"]
            w_issue()

        FFN_PARTS = [(0, 8), (8, 8), (16, 6)]
        for l in range(NL):
            wq.append((wview(w_in, l, 0, 256), KC, 256))
            wq.append((wview(w_in, l, 256, 256), KC, 256))
            for j in range(NTILE):
                for b in range(10):
                    wq.append((wview(w_in, l, b * 256, 256), KC, 256))
                wq.append((wview(w_in, l, 2560, 16), KC, 16))
            for j in range(NTILE):
                for b in range(4):
                    wq.append((wview(w_out, l, b * 256, 256), KC, 256))
                for (m0, nm) in FFN_PARTS:
                    for mp in range(nm // 2):
                        wq.append((wview(w_fi, l, (m0 + 2 * mp) * 128, 256), KC, 256))
                        wq.append((wview(w_fi, l, DFF + (m0 + 2 * mp) * 128, 256), KC, 256))
                    for b in range(4):
                        wq.append((wview(w_fo, l, b * 256, 256, k0=m0, nk=nm), nm, 256))

        dma_c("sp", pvec[:], pvec_d, writes=[Bpv])
        dma_c("sp", fnorm[:], fnorm_d, writes=[Bpv])
        dma_c("sp", Um[:], U_d, writes=[Bconst])
        dma_c("sp", Em[:], E_d, writes=[Bconst])
        dma_c("sp", masks[:], masks_d, writes=[Bconst])
        Bpv.w = (s_c, 16 * cnt["c"]); Bconst.w = (s_c, 16 * cnt["c"])
        for k in range(KC):
            fw.dma("sp", lambda h, k=k: h.dma_start(out=xres[:, k, :], in_=xT[:, k, :]), s_x, 16 * (k + 1),
                   writes=Bx[k])
        for k in range(KC):
            for b in Bx[k]:
                b.w = (s_x, 16 * KC)
        fw.op("dve", lambda h: h.memset(ones_b[:], 1.0), writes=[Bones])
        fw.op("dve", lambda h: h.memset(epsc[:], EPS), writes=[Bconst])
        fw.op("dve", lambda h: h.memset(zcol[:], 0.0), writes=[Bconst])

        def pv(l, col, n=1):
            return pvec[:, l * NPV + col:l * NPV + col + n]

        def rmsnorm(gain_ap_fn, j, t0, n):
            for k in range(KC):
                act(hid[:, k, 0:n], xres[:, k, t0:t0 + n], AF.Square, reads=[Bx[k][j]], writes=[Bhid[k]])
            bank, Bb = rot()
            for k in range(KC):
                mm(bank[:, 0:n], ones_b[:], hid[:, k, 0:n], k == 0, k == KC - 1, reads=[Bones, Bhid[k]], writes=[Bb], signal=(k == KC - 1))
            rs = tmp[:, 8, 0:n]
            act(rs, bank[:, 0:n], AF.Sqrt, reads=[Bb, Bconst], writes=[Bt[8]], bias=epsc[:], scale=1.0 / D)
            fw.op("dve", lambda h: h.reciprocal(out=rs, in_=rs), reads=[Bt[8]], writes=[Bt[8]])
            for k in range(KC):
                stt(hb[:, k, 0:n], xres[:, k, t0:t0 + n], gain_ap_fn(k), rs, ALU.mult, ALU.mult,
                    reads=[Bx[k][j], Bt[8], Bpv], writes=[Bhb[k]])

        cc_sems = []
        for l in range(NL):
            for g in range(8):
                fw.dma("pool", lambda h, g=g, l=l: h.dma_start(out=bdb[:, g, :], in_=bd_d[:, l, g, :]), s_bd, 16 * (8 * l + g + 1), writes=[Bbd])
            Bbd.w = (s_bd, 16 * 8 * (l + 1))
            fw.dma("sp", lambda h, l=l: h.dma_start(out=waug[:], in_=waug_d[:, l, :]), s_wa, 16 * (l + 1), writes=[Bwaug])
            act(clam[:, 0:4], pv(l, PV_LAM, 4), AF.Exp, reads=[Bpv], writes=[Bclam], scale=-1.0)
            act(clam[:, 0:4], clam[:, 0:4], AF.Ln, reads=[Bclam], writes=[Bclam], bias=1.0)
            ts(clam[:, 4:8], clam[:, 0:4], -16.0, None, ALU.mult, None, reads=[Bclam], writes=[Bclam])
            ts(clam[:, 0:4], clam[:, 0:4], -8.0, None, ALU.mult, None, reads=[Bclam], writes=[Bclam])

            rmsnorm(lambda k: pv(l, PV_G1 + k), NTILE - 1, NT - 3, 3)
            for blk in range(2):
                wv, Bwv = wnext()
                for c2 in range(2):
                    ct = blk * 2 + c2
                    for k in range(KC):
                        mm(GS[:, 16 + ct * 3:16 + ct * 3 + 3], wv[:, k, c2 * 128:(c2 + 1) * 128], hb[:, k, 0:3], k == 0, k == KC - 1,
                           reads=[Bwv, Bhb[k]], writes=[Bgs], signal=(k == KC - 1))
                wrel()
            fw.op("dve", lambda h: h.tensor_copy(out=tails[:], in_=GS[:, 16:28]), reads=[Bgs], writes=[Btails])
            cnt["misc"] += 1
            Btl_in, Btl_out = Buf("tlin"), Buf("tlout")
            fw.dma("pool", lambda h, l=l: h.dma_start(out=tl_in[l].ap(), in_=tails[:]), s_misc, 16 * cnt["misc"], reads=[Btails], writes=[Btl_in])
            cs = fw.new_sem(f"cc_t{l}")
            fw.dma("pool", lambda h, l=l: h.collective_compute("AllGather", ALU.bypass, replica_groups=[list(range(NCORES))],
                                                                 ins=[tl_in[l].ap().opt()], outs=[tl_out[l].ap().opt()]),
                   cs, 1, reads=[Btl_in], writes=[Btl_out], inc=None)
            cnt["misc"] += 1
            fw.dma("pool", lambda h, l=l: h.dma_start(out=tgath[:], in_=tl_out[l].ap().rearrange("(r p) w -> p r w", p=128)),
                   s_misc, 16 * cnt["misc"], reads=[Btl_out], writes=[Btg])
            xh = xpre[:, :, 0:3]
            ts(tails[:], tgath[:, 0, :], masks[:, 8:9], None, ALU.mult, None, reads=[Btg, Bconst], writes=[Btails])
            for r in range(1, NCORES):
                stt(tails[:], tgath[:, r, :], masks[:, 8 + r:9 + r], tails[:], ALU.mult, ALU.add, reads=[Btg, Bconst, Btails], writes=[Btails])
            fw.op("dve", lambda h: h.tensor_copy(out=xh, in_=tails[:].rearrange("p (c t) -> p c t", t=3)), reads=[Btails], writes=Bxpre)

            rot_state["lst"] = [0, 1, 2, 3]
            for j in range(NTILE):
                t0 = j * TT
                rmsnorm(lambda k: pv(l, PV_G1 + k), j, t0, TT)
                wblocks = [wnext() for _ in range(2)]
                px = []
                for ct in range(4):
                    wv, Bwv = wblocks[ct // 2]
                    bank, Bb = rot()
                    for k in range(KC):
                        mm(bank[:], wv[:, k, (ct % 2) * 128:(ct % 2 + 1) * 128], hb[:, k, :], k == 0, k == KC - 1,
                           reads=[Bwv, Bhb[k]], writes=[Bb], signal=(k == KC - 1))
                    act(xpre[:, ct, 3:3 + TT], bank[:], AF.Copy, reads=[Bb], writes=[Bxpre[ct]])
                wrel(2)
                wblocks = [wnext() for _ in range(2)]
                for ct in range(4):
                    xc, Bxc = tmp[:, 0, :], Bt[0]
                    cw = lambda jj: pv(l, PV_CW + jj * 4 + ct)
                    ts(xc, xpre[:, ct, 0:TT], cw(0), pv(l, PV_CB + ct), ALU.mult, ALU.add, reads=[Bxpre[ct], Bpv], writes=[Bxc])
                    for jj in range(1, 4):
                        stt(xc, xpre[:, ct, jj:jj + TT], cw(jj), xc, ALU.mult, ALU.add, reads=[Bxpre[ct], Bpv, Bxc], writes=[Bxc])
                    fw.op("dve", lambda h, ct=ct: h.tensor_copy(out=xpre[:, ct, 0:3], in_=xpre[:, ct, TT:TT + 3]), reads=[Bxpre[ct]], writes=[Bxpre[ct]])
                    xcb = tmp[:, 1, :].bitcast(BF16)[:, 0:TT]
                    Bxcb = Bt[1]
                    act(xcb, xc, AF.Copy, reads=[Bxc], writes=[Bxcb])
                    bank_r, Bbr = rot()
                    mm(bank_r[:], bdb[:, ct, :], xcb, True, True, reads=[Bbd, Bxcb], writes=[Bbr], signal=True)
                    bank_i, Bbi = rot()
                    mm(bank_i[:], bdb[:, 4 + ct, :], xcb, True, True, reads=[Bbd, Bxcb], writes=[Bbi], signal=True)
                    r_t, Br = tmp[:, 2, :], Bt[2]
                    i_t, Bi = tmp[:, 3, :], Bt[3]
                    a_t, Ba = tmp[:, 4, :], Bt[4]
                    s_t, Bs = tmp[:, 5, :], Bt[5]
                    hl, Bhl = tmp[:, 6, :], Bt[6]
                    Ac, BAc = tmp[:, 7, :], Bt[7]
                    act(r_t, bank_r[:], AF.Sigmoid, reads=[Bbr, Bpv], writes=[Br], bias=pv(l, PV_BA + ct))
                    act(i_t, bank_i[:], AF.Sigmoid, reads=[Bbi, Bpv], writes=[Bi], bias=pv(l, PV_BI + ct))
                    act(a_t, r_t, AF.Exp, reads=[Br, Bclam], writes=[Ba], scale=clam[:, ct:ct + 1])
                    act(s_t, r_t, AF.Exp, reads=[Br, Bclam], writes=[Bs], scale=clam[:, 4 + ct:5 + ct])
                    act(s_t, s_t, AF.Sqrt, reads=[Bs], writes=[Bs], scale=-1.0, bias=1.0)
                    tt(s_t, s_t, i_t, ALU.mult, reads=[Bs, Bi], writes=[Bs])
                    tt(s_t, s_t, xc, ALU.mult, reads=[Bs, Bxc], writes=[Bs])
                    h0 = 0.0 if j == 0 else carry[:, ct:ct + 1]
                    A0 = 1.0 if j == 0 else carry[:, 4 + ct:5 + ct]
                    fw.op("dve", lambda h, h0=h0, hl=hl, a_t=a_t, s_t=s_t: h.tensor_tensor_scan(out=hl, data0=a_t, data1=s_t, initial=h0, op0=ALU.mult, op1=ALU.add),
                          reads=[Ba, Bs, Bcarry], writes=[Bhl])
                    fw.op("dve", lambda h, A0=A0, Ac=Ac, a_t=a_t: h.tensor_tensor_scan(out=Ac, data0=a_t, data1=zcol[:, 0:1].to_broadcast([128, TT]), initial=A0, op0=ALU.mult, op1=ALU.add),
                          reads=[Ba, Bcarry, Bconst], writes=[BAc])
                    fw.op("dve", lambda h, ct=ct, hl=hl: h.tensor_copy(out=carry[:, ct:ct + 1], in_=hl[:, TT - 1:TT]), reads=[Bhl], writes=[Bcarry])
                    fw.op("dve", lambda h, ct=ct, Ac=Ac: h.tensor_copy(out=carry[:, 4 + ct:5 + ct], in_=Ac[:, TT - 1:TT]), reads=[BAc], writes=[Bcarry])
                    wv, Bwv = wblocks[ct // 2]
                    bank, Bb = rot()
                    for k in range(KC):
                        mm(bank[:], wv[:, k, (ct % 2) * 128:(ct % 2 + 1) * 128], hb[:, k, :], k == 0, k == KC - 1,
                           reads=[Bwv, Bhb[k]], writes=[Bb], signal=(k == KC - 1))
                    gl, Bgl = tmp[:, 2, :], Bt[2]
                    act(gl, bank[:], AF.Gelu_apprx_tanh, reads=[Bb], writes=[Bgl])
                    tt(hg[:, ct, t0:t0 + TT], hl, gl, ALU.mult, reads=[Bhl, Bgl], writes=[Bhg[ct][j]])
                    tt(Ag[:, ct, t0:t0 + TT], Ac, gl, ALU.mult, reads=[BAc, Bgl], writes=[BAg[ct][j]])
                wrel(2)

                wv, Bwv = wnext()
                for ft in range(2):
                    bank, Bb = rot()
                    for k in range(KC):
                        mm(bank[:], wv[:, k, ft * 128:(ft + 1) * 128], hb[:, k, :], k == 0, k == KC - 1,
                           reads=[Bwv, Bhb[k]], writes=[Bb], signal=(k == KC - 1))
                    act(qb[:, ft, t0:t0 + TT], bank[:], AF.Copy, reads=[Bb], writes=[Bqb[ft][j]], scale=0.125)
                wrel()
                wk, Bwk = wnext()
                wv0, Bwv0 = wnext()
                wv1, Bwv1 = wnext()
                kraw = [tmp[:, 3 + s // 2, (s % 2) * 256:(s % 2) * 256 + 256] for s in range(4)]
                Bkraw = [Bt[3 + s // 2] for s in range(4)]
                vbs = [tmp[:, 5 + s // 2, :].bitcast(BF16)[:, (s % 2) * 512:(s % 2) * 512 + 512] for s in range(4)]
                Bvbs = [Bt[5 + s // 2] for s in range(4)]
                for s in range(4):
                    for k in range(KC):
                        mm(G0[:, 0:256], hb[:, k, s * 128:(s + 1) * 128], wk[:, k, :], k == 0, k == KC - 1,
                           reads=[Bwk, Bhb[k]], writes=[Bg0a], signal=(k == KC - 1))
                    fw.op("dve", lambda h, s=s: h.tensor_copy(out=kraw[s], in_=G0[:, 0:256]), reads=[Bg0a], writes=[Bkraw[s]])
                    for k in range(KC):
                        mm(G1[:, 0:256], hb[:, k, s * 128:(s + 1) * 128], wv0[:, k, :], k == 0, k == KC - 1,
                           reads=[Bwv0, Bhb[k]], writes=[Bg1], signal=False)
                    for k in range(KC):
                        mm(G1[:, 256:512], hb[:, k, s * 128:(s + 1) * 128], wv1[:, k, :], k == 0, k == KC - 1,
                           reads=[Bwv1, Bhb[k]], writes=[Bg1], signal=(k == KC - 1))
                    act(vbs[s], G1[:], AF.Copy, reads=[Bg1], writes=[Bvbs[s]])
                wrel(3)
                for gb in range(2):
                    wv, Bwv = wnext()
                    for c2 in range(2):
                        ct = gb * 2 + c2
                        bank, Bb = rot()
                        for k in range(KC):
                            mm(bank[:], wv[:, k, c2 * 128:(c2 + 1) * 128], hb[:, k, :], k == 0, k == KC - 1,
                               reads=[Bwv, Bhb[k]], writes=[Bb], signal=(k == KC - 1))
                        act(sg[:, ct, t0:t0 + TT], bank[:], AF.Silu, reads=[Bb], writes=[Bsg[ct][j]])
                    wrel()
                wv, Bwv = wnext()
                bank, Bb = rot()
                for k in range(KC):
                    mm(bank[0:16, :], wv[:, k, 0:16], hb[:, k, :], k == 0, k == KC - 1,
                       reads=[Bwv, Bhb[k]], writes=[Bb], signal=(k == KC - 1))
                wrel()
                fw.op("dve", lambda h: h.memset(zaug, 1.0), writes=[Bzaug])
                fw.op("dve", lambda h, bank=bank: h.tensor_copy(out=zaug[0:16, :], in_=bank[0:16, :]), reads=[Bb], writes=[Bzaug])
                for s in range(4):
                    mm(GS[:, 256:512], zaug[0:17, s * 128:(s + 1) * 128], waug[:, :], True, True, reads=[Bzaug, Bwaug], writes=[Bgs], signal=True)
                    act(spt, GS[:, 256:512], AF.Exp, reads=[Bgs], writes=[Bsp], scale=-1.0)
                    act(spt, spt, AF.Ln, reads=[Bsp], writes=[Bsp], bias=1.0)
                    mm(GS[:, 256:512], Um[:], spt, True, True, reads=[Bconst, Bsp], writes=[Bgs], signal=True)
                    act(dect, GS[:, 256:512], AF.Exp, reads=[Bgs], writes=[Bdec], scale=-1.0 / 16.0)
                    tt(kdec[:], kraw[s], dect, ALU.mult, reads=[Bkraw[s], Bdec], writes=[Bkdec])
                    for ft in range(2):
                        mm(GS[:, ft * 2:ft * 2 + 2], spt[:, ft * 128:(ft + 1) * 128], Em[:], True, True, reads=[Bsp, Bconst], writes=[Bgs], signal=(ft == 1))
                    c0 = j * 8 + s * 2
                    act(dcy[:, :, c0:c0 + 2], GS[:, 0:4].rearrange("p (f c) -> p f c", c=2), AF.Exp, reads=[Bgs], writes=[Bdcy], scale=-1.0 / 16.0)
                    for cc in range(2):
                        c = c0 + cc
                        for ft in range(2):
                            mm(G2[:, ft * 256:(ft + 1) * 256], kdec[cc * 64:(cc + 1) * 64, ft * 128:(ft + 1) * 128],
                               vbs[s][cc * 64:(cc + 1) * 64, ft * 256:(ft + 1) * 256], True, True,
                               reads=[Bkdec, Bvbs[s]], writes=[Bg2], signal=(ft == 1))
                        for ft in range(2):
                            if c == 0:
                                fw.op("dve", lambda h, ft=ft: h.tensor_copy(out=Sst[:, ft, :], in_=G2[:, ft * 256:(ft + 1) * 256]), reads=[Bg2], writes=[BS[ft]])
                            else:
                                stt(Sst[:, ft, :], Sst[:, ft, :], dcy[:, ft, c:c + 1], G2[:, ft * 256:(ft + 1) * 256], ALU.mult, ALU.add,
                                    reads=[BS[ft], Bdcy, Bg2], writes=[BS[ft]])
                            for h2 in range(2):
                                act(Sloc[h2 * 64:(h2 + 1) * 64, ft, c, :], Sst[h2 * 64:(h2 + 1) * 64, ft, h2 * 128:(h2 + 1) * 128], AF.Copy,
                                    reads=[BS[ft]], writes=[BSloc[ft][c]])

            for ft in range(2):
                fw.op("dve", lambda h, ft=ft: h.tensor_tensor_scan(out=Dcum[:, ft, :], data0=dcy[:, ft, :], data1=zcol[:, 0:1].to_broadcast([128, NCH]),
                                                                    initial=1.0, op0=ALU.mult, op1=ALU.add), reads=[Bdcy, Bconst], writes=[BDcum])
            fw.op("dve", lambda h: h.tensor_copy(out=xch[:, 0:8], in_=carry[:, 0:8]), reads=[Bcarry], writes=[Bxch])
            for ft in range(2):
                for h2 in range(2):
                    fw.op("dve", lambda h, ft=ft, h2=h2: h.tensor_copy(out=xch[h2 * 64:(h2 + 1) * 64, 8 + ft * 128:8 + (ft + 1) * 128],
                                                                         in_=Sst[h2 * 64:(h2 + 1) * 64, ft, h2 * 128:(h2 + 1) * 128]),
                          reads=[BS[ft]], writes=[Bxch])
            fw.op("dve", lambda h: h.tensor_copy(out=xch[:, 264:266], in_=Dcum[:, :, NCH - 1]), reads=[BDcum], writes=[Bxch])
            Bag_in, Bag_out = Buf("agin"), Buf("agout")
            cnt["misc"] += 1
            fw.dma("pool", lambda h, l=l: h.dma_start(out=ag_in[l].ap(), in_=xch[:]), s_misc, 16 * cnt["misc"], reads=[Bxch], writes=[Bag_in])
            cs = fw.new_sem(f"cc_s{l}")
            fw.dma("pool", lambda h, l=l: h.collective_compute("AllGather", ALU.bypass, replica_groups=[list(range(NCORES))],
                                                                 ins=[ag_in[l].ap().opt()], outs=[ag_out[l].ap().opt()]),
                   cs, 1, reads=[Bag_in], writes=[Bag_out], inc=None)
            gath = tmp[:, 0:5, :].rearrange("p a b -> p (a b)")[:, 0:NCORES * XW].rearrange("p (r w) -> p r w", w=XW)
            Bgath = Bt[0:5]
            cnt["misc"] += 1
            fw.dma("pool", lambda h, l=l: h.dma_start(out=gath, in_=ag_out[l].ap().rearrange("(r p) w -> p r w", p=128)),
                   s_misc, 16 * cnt["misc"], reads=[Bag_out], writes=Bgath)
            fw.op("dve", lambda h: h.memset(init[:], 0.0), writes=[Binit])
            for r in range(NCORES):
                m = masks[:, r:r + 1]
                ts(cmb[:, 0:4], gath[:, r, 0:4], m, None, ALU.mult, None, reads=Bgath + [Bconst], writes=[Bcmb])
                ts(cmb[:, 4:260], gath[:, r, 8:264], m, None, ALU.mult, None, reads=Bgath + [Bconst], writes=[Bcmb])
                om = masks[:, 16 + r:17 + r]
                stt(cmb[:, 264:268], gath[:, r, 4:8], m, om.to_broadcast([128, 4]), ALU.mult, ALU.add, reads=Bgath + [Bconst], writes=[Bcmb])
                stt(cmb[:, 268:270], gath[:, r, 264:266], m, om.to_broadcast([128, 2]), ALU.mult, ALU.add, reads=Bgath + [Bconst], writes=[Bcmb])
                tt(init[:, 0:4], init[:, 0:4], cmb[:, 264:268], ALU.mult, reads=[Binit, Bcmb], writes=[Binit])
                tt(init[:, 0:4], init[:, 0:4], cmb[:, 0:4], ALU.add, reads=[Binit, Bcmb], writes=[Binit])
                for ft in range(2):
                    stt(init[:, 4 + ft * 128:4 + (ft + 1) * 128], init[:, 4 + ft * 128:4 + (ft + 1) * 128], cmb[:, 268 + ft:269 + ft],
                        cmb[:, 4 + ft * 128:4 + (ft + 1) * 128], ALU.mult, ALU.add, reads=[Binit, Bcmb], writes=[Binit])
            fw.op("dve", lambda h: h.tensor_copy(out=Sinb[:], in_=init[:, 4:260].rearrange("p (f v) -> p f v", v=128)), reads=[Binit], writes=[BSinb])

            rot_state["lst"] = [0, 1, 2, 3, 5, 6]
            for j in range(NTILE):
                t0 = j * TT
                for ct in range(4):
                    stt(hb[:, ct, :], Ag[:, ct, t0:t0 + TT], init[:, ct:ct + 1], hg[:, ct, t0:t0 + TT], ALU.mult, ALU.add,
                        reads=[BAg[ct][j], Bhg[ct][j], Binit], writes=[Bhb[ct]])
                qD = tmp[:, 0, :].bitcast(BF16)
                BqD = Bt[0]
                for ft in range(2):
                    tt(qD[:, ft * TT:(ft + 1) * TT].rearrange("p (c k) -> p c k", k=64),
                       qb[:, ft, t0:t0 + TT].rearrange("p (c k) -> p c k", k=64),
                       Dcum[:, ft, j * 8:(j + 1) * 8].unsqueeze(2).to_broadcast([128, 8, 64]), ALU.mult,
                       reads=[Bqb[ft][j], BDcum], writes=[BqD])
                for hd in range(4):
                    ft, h2 = hd // 2, hd % 2
                    pr = slice(h2 * 64, (h2 + 1) * 64)
                    bank, Bb = rot()
                    for cc in range(8):
                        c = j * 8 + cc
                        mm(bank[:, cc * 64:(cc + 1) * 64], Sloc[pr, ft, c, :], qb[pr, ft, t0 + cc * 64:t0 + (cc + 1) * 64], True, False,
                           reads=[BSloc[ft][c], Bqb[ft][j]], writes=[Bb], signal=False)
                        mm(bank[:, cc * 64:(cc + 1) * 64], Sinb[pr, ft, :], qD[pr, ft * TT + cc * 64:ft * TT + (cc + 1) * 64], False, True,
                           reads=[BSinb, BqD], writes=[Bb], signal=(cc == 7))
                    osq = tmp[:, 1, :].bitcast(BF16)[:, 0:TT]
                    act(osq, bank[:], AF.Square, reads=[Bb], writes=[Bt[1]])
                    bank2, Bb2 = rot()
                    mm(bank2[:], ones_b[:], osq, True, True, reads=[Bones, Bt[1]], writes=[Bb2], signal=True)
                    rs = tmp[:, 2, :]
                    act(rs, bank2[:], AF.Sqrt, reads=[Bb2, Bconst], writes=[Bt[2]], bias=epsc[:], scale=1.0 / 128.0)
                    fw.op("dve", lambda h, rs=rs: h.reciprocal(out=rs, in_=rs), reads=[Bt[2]], writes=[Bt[2]])
                    y = tmp[:, 3, :]
                    stt(y, bank[:], pv(l, PV_GN), rs, ALU.mult, ALU.mult, reads=[Bb, Bt[2], Bpv], writes=[Bt[3]])
                    tt(hb[:, 4 + hd, :], y, sg[:, hd, t0:t0 + TT], ALU.mult, reads=[Bt[3], Bsg[hd][j]], writes=[Bhb[4 + hd]])
                for b in range(4):
                    wv, Bwv = wnext()
                    for c2 in range(2):
                        m = b * 2 + c2
                        bank, Bb = rot()
                        for k in range(KC):
                            mm(bank[:], wv[:, k, c2 * 128:(c2 + 1) * 128], hb[:, k, :], k == 0, k == KC - 1,
                               reads=[Bwv, Bhb[k]], writes=[Bb], signal=(k == KC - 1))
                        tt(xres[:, m, t0:t0 + TT], xres[:, m, t0:t0 + TT], bank[:], ALU.add, reads=[Bx[m][j], Bb], writes=[Bx[m][j]])
                    wrel()
                rmsnorm(lambda k: pv(l, PV_G2 + k), j, t0, TT)
                for (m0, nm) in FFN_PARTS:
                    for mp in range(nm // 2):
                        wg, Bwg = wnext()
                        wu, Bwu = wnext()
                        for c2 in range(2):
                            jj = mp * 2 + c2
                            bg, Bbg = rot()
                            for k in range(KC):
                                mm(bg[:], wg[:, k, c2 * 128:(c2 + 1) * 128], hb[:, k, :], k == 0, k == KC - 1,
                                   reads=[Bwg, Bhb[k]], writes=[Bbg], signal=(k == KC - 1))
                            bu, Bbu = rot()
                            for k in range(KC):
                                mm(bu[:], wu[:, k, c2 * 128:(c2 + 1) * 128], hb[:, k, :], k == 0, k == KC - 1,
                                   reads=[Bwu, Bhb[k]], writes=[Bbu], signal=(k == KC - 1))
                            sgt = tmp[:, 4 + (jj % 2), :]
                            Bsgt = Bt[4 + (jj % 2)]
                            act(sgt, bg[:], AF.Silu, reads=[Bbg], writes=[Bsgt])
                            tt(hid[:, jj, :], sgt, bu[:], ALU.mult, reads=[Bsgt, Bbu], writes=[Bhid[jj]])
                        wrel(2)
                    for b in range(4):
                        wv, Bwv = wnext()
                        for c2 in range(2):
                            m = b * 2 + c2
                            bank, Bb = rot()
                            for k in range(nm):
                                mm(bank[:], wv[:, k, c2 * 128:(c2 + 1) * 128], hid[:, k, :], k == 0, k == nm - 1,
                                   reads=[Bwv, Bhid[k]], writes=[Bb], signal=(k == nm - 1))
                            tt(xres[:, m, t0:t0 + TT], xres[:, m, t0:t0 + TT], bank[:], ALU.add, reads=[Bx[m][j], Bb], writes=[Bx[m][j]])
                        wrel()

        rot_state["lst"] = [0, 1, 2, 3, 5, 6]
        evs = []
        nout = 0
        for j in range(NTILE):
            t0 = j * TT
            for k in range(KC):
                act(hid[:, k, :], xres[:, k, t0:t0 + TT], AF.Square, reads=[Bx[k][j]], writes=[Bhid[k]])
            bank, Bb = rot()
            for k in range(KC):
                mm(bank[:], ones_b[:], hid[:, k, :], k == 0, k == KC - 1, reads=[Bones, Bhid[k]], writes=[Bb], signal=(k == KC - 1))
            rs = tmp[:, 8, :]
            act(rs, bank[:], AF.Sqrt, reads=[Bb, Bconst], writes=[Bt[8]], bias=epsc[:], scale=1.0 / D)
            fw.op("dve", lambda h, rs=rs: h.reciprocal(out=rs, in_=rs), reads=[Bt[8]], writes=[Bt[8]])
            for k in range(KC):
                stt(xres[:, k, t0:t0 + TT], xres[:, k, t0:t0 + TT], fnorm[:, k:k + 1], rs, ALU.mult, ALU.mult,
                    reads=[Bx[k][j], Bt[8], Bpv], writes=[Bx[k][j]])
                nout += 1
                evs.append(fw.dma("sp", lambda h, k=k, t0=t0: h.dma_start(out=yT[:, k, t0:t0 + TT], in_=xres[:, k, t0:t0 + TT]),
                                  s_out, 16 * nout, reads=[Bx[k][j]]))
        fw.wait_only("sp", [(s_out, 16 * nout)])
        assert wstate["used"] == len(wq) == wstate["released"], (wstate, len(wq))
        fw.finish()
    return nc


def host_inputs(inputs, NT, NL, ncores=NCORES):
    f = lambda a: np.ascontiguousarray(np.asarray(a, dtype=np.float32))
    x = f(inputs["x"])[0]
    S = x.shape[0]
    assert S == NT * ncores
    pvec = np.zeros((128, NL, NPV), np.float32)

    def col(v, n):
        return f(v).reshape(n, 128).T
    for l in range(NL):
        pvec[:, l, PV_G1:PV_G1 + 8] = col(inputs["norm1"][l], 8)
        pvec[:, l, PV_G2:PV_G2 + 8] = col(inputs["norm2"][l], 8)
        for jj in range(4):
            pvec[:, l, PV_CW + jj * 4:PV_CW + jj * 4 + 4] = col(inputs["conv_w"][l][jj], 4)
        pvec[:, l, PV_CB:PV_CB + 4] = col(inputs["conv_b"][l], 4)
        pvec[:, l, PV_BA:PV_BA + 4] = col(inputs["lru_ba"][l], 4)
        pvec[:, l, PV_BI:PV_BI + 4] = col(inputs["lru_bi"][l], 4)
        pvec[:, l, PV_LAM:PV_LAM + 4] = col(inputs["lru_lambda"][l], 4)
        pvec[:, l, PV_GN] = f(inputs["gla_norm"][l])
    pvec = pvec.reshape(128, NL * NPV)
    fnorm = col(inputs["final_norm"], 8)
    bd = np.zeros((128, NL, 8, 128), np.float32)
    wa, wi = f(inputs["lru_wa"]), f(inputs["lru_wi"])
    for l in range(NL):
        for ct in range(4):
            for h2 in range(2):
                bd[h2 * 64:(h2 + 1) * 64, l, ct, h2 * 64:(h2 + 1) * 64] = wa[l, ct * 2 + h2]
                bd[h2 * 64:(h2 + 1) * 64, l, 4 + ct, h2 * 64:(h2 + 1) * 64] = wi[l, ct * 2 + h2]
    waug = np.zeros((17, NL, 256), np.float32)
    for l in range(NL):
        waug[0:16, l] = f(inputs["gla_w_alpha"][l])
        waug[16, l] = f(inputs["gla_b_alpha"][l])
    idx = np.arange(128)
    U = ((idx[:, None] > idx[None, :]) & ((idx[:, None] // 64) == (idx[None, :] // 64))).astype(np.float32)
    E = (idx[:, None] // 64 == np.arange(2)[None, :]).astype(np.float32)
    shared = {
        "w_in": f(inputs["w_in"])[:NL], "w_out": f(inputs["w_out"])[:NL],
        "w_ffn_in": f(inputs["w_ffn_in"])[:NL], "w_ffn_out": f(inputs["w_ffn_out"])[:NL],
        "pvec": pvec, "fnorm": np.ascontiguousarray(fnorm), "bd": bd, "waug": waug, "Umat": U, "Emat": E,
    }
    in_maps = []
    for c in range(ncores):
        xs = x[c * NT:(c + 1) * NT]
        xTc = np.ascontiguousarray(xs.T.reshape(KC, 128, NT).transpose(1, 0, 2))
        masks = np.zeros((128, 24), np.float32)
        masks[:, 0:8] = (np.arange(8) < c).astype(np.float32)[None, :]
        masks[:, 16:24] = 1.0 - masks[:, 0:8]
        if c > 0:
            masks[:, 8 + c - 1] = 1.0
        m = dict(shared)
        m["xT"] = xTc
        m["masks"] = masks
        in_maps.append(m)
    return in_maps


def run(inputs, NT, NL, trace=False):
    nc = build(NT, NL)
    in_maps = host_inputs(inputs, NT, NL)
    res = run_bass_kernel_spmd(nc, in_maps, core_ids=list(range(NCORES)))
    outs = []
    for c in range(NCORES):
        yTc = res.results[c]["yT"]
        outs.append(yTc.transpose(2, 1, 0).reshape(NT, D))
    return np.concatenate(outs, axis=0)[None].astype(np.float32)


def kernel(**inputs):
    return run(inputs, 2048, 4)
```

```python
import numpy as np
from contextlib import ExitStack
import concourse.bass as bass
import concourse.mybir as mybir
from concourse.bass_utils import run_bass_kernel_spmd

F32 = mybir.dt.float32
BF16 = mybir.dt.bfloat16
ALU = mybir.AluOpType
AF = mybir.ActivationFunctionType

NCORES = 8
D = 1024
KC = 8
TT = 512
PIN = 2576
DFF = 2816
EPS = 1e-6
PV_G1, PV_G2, PV_CW, PV_CB, PV_BA, PV_BI, PV_LAM, PV_GN, NPV = 0, 8, 16, 32, 36, 40, 44, 48, 49
XW = 266
SLOT_ELEMS = 2048
NSLOT = 4
USE_SCRATCH = True


class Buf:
    __slots__ = ("name", "w", "r", "psum")

    def __init__(self, name, psum=False):
        self.name = name
        self.w = None
        self.r = []
        self.psum = psum


class FW:
    ENG = ("pe", "act", "dve", "pool", "sp")

    def __init__(self, nc, es):
        self.nc = nc
        self.es = es
        self.sem = {e: es.enter_context(nc.semaphore("sem_" + e)) for e in self.ENG}
        self.cnt = {e: 0 for e in self.ENG}
        self.prog = {e: [] for e in self.ENG}
        self.seen = {e: {} for e in self.ENG}

    def new_sem(self, name):
        return self.es.enter_context(self.nc.semaphore(name))

    def _collect(self, eng, reads, writes, extra):
        deps = []
        for b in reads:
            if b.w is not None:
                deps.append(b.w)
            if b.psum:
                deps.extend(b.r)
        for b in writes:
            if b.w is not None:
                deps.append(b.w)
            deps.extend(b.r)
        deps.extend(extra)
        waits = {}
        mysem = self.sem[eng]
        for (s, v) in deps:
            if eng == "pe" and s is mysem:
                continue
            if self.seen[eng].get(id(s), 0) >= v:
                continue
            if waits.get(id(s), (None, 0))[1] < v:
                waits[id(s)] = (s, v)
        for (s, v) in waits.values():
            self.seen[eng][id(s)] = v
        return list(waits.values())

    def op(self, eng, fn, reads=(), writes=(), extra=(), signal=True):
        waits = self._collect(eng, reads, writes, extra)
        sem = self.sem[eng]
        if signal:
            self.cnt[eng] += 1
            ev = (sem, self.cnt[eng])
        else:
            ev = (sem, self.cnt[eng] + 1)
        self.prog[eng].append(("op", fn, waits, self.cnt[eng] if signal else None, None, None))
        for b in reads:
            b.r.append(ev)
        for b in writes:
            b.w = ev
            b.r = []
        return ev

    def dma(self, eng, fn, dsem, dval, reads=(), writes=(), extra=(), inc=16):
        waits = self._collect(eng, reads, writes, extra)
        ev = (dsem, dval)
        self.prog[eng].append(("dma", fn, waits, None, dsem, inc))
        for b in reads:
            b.r.append(ev)
        for b in writes:
            b.w = ev
            b.r = []
        return ev

    def wait_only(self, eng, events):
        self.prog[eng].append(("wait", None, list(events), None, None, None))

    def finish(self):
        engsem = {id(self.sem[e]): e for e in self.ENG}
        waited = {e: set() for e in self.ENG}
        for e in self.ENG:
            for (_k, _f, waits, _c, _d, _i) in self.prog[e]:
                for (s, v) in waits:
                    if id(s) in engsem:
                        waited[engsem[id(s)]].add(v)
        remap = {e: {v: i + 1 for i, v in enumerate(sorted(waited[e]))} for e in self.ENG}
        self.n_signals = {e: len(remap[e]) for e in self.ENG}

        def run(eng, h):
            sem = self.sem[eng]
            for (kind, fn, waits, c, dsem, inc) in self.prog[eng]:
                for (s, v) in waits:
                    if id(s) in engsem:
                        v = remap[engsem[id(s)]][v]
                    h.wait_ge(s, v)
                if kind == "wait":
                    continue
                ins = fn(h)
                if kind == "op":
                    if c is not None and c in remap[eng]:
                        ins.then_inc(sem, 1)
                else:
                    if inc is None:
                        ins.then_inc(dsem)
                    else:
                        ins.then_inc(dsem, inc)

        with self.nc.Block() as block:
            @block.tensor
            def _(h):
                run("pe", h)

            @block.scalar
            def _(h):
                run("act", h)

            @block.vector
            def _(h):
                run("dve", h)

            @block.gpsimd
            def _(h):
                run("pool", h)

            @block.sync
            def _(h):
                run("sp", h)


def build(NT=2048, NL=4):
    NTILE = NT // TT
    NCH = NT // 64
    nc = bass.Bass("TRN2", target_bir_lowering=False)
    di = lambda name, shape: nc.dram_tensor(name, shape, F32, kind="ExternalInput").ap()
    xT = di("xT", [128, KC, NT])
    w_in = di("w_in", [NL, D, PIN])
    w_out = di("w_out", [NL, D, D])
    w_fi = di("w_ffn_in", [NL, D, 2 * DFF])
    w_fo = di("w_ffn_out", [NL, DFF, D])
    pvec_d = di("pvec", [128, NL * NPV])
    fnorm_d = di("fnorm", [128, KC])
    bd_d = di("bd", [128, NL, 8, 128])
    waug_d = di("waug", [17, NL, 256])
    U_d = di("Umat", [128, 128])
    E_d = di("Emat", [128, 2])
    masks_d = di("masks", [128, 24])
    xh_d = di("xhalo", [128, KC, 3])
    yT = nc.dram_tensor("yT", [128, KC, NT], F32, kind="ExternalOutput").ap()
    ag_in = [nc.dram_tensor(f"agin{l}", [128, XW], F32) for l in range(NL)]
    ag_out = [nc.dram_tensor(f"agout{l}", [NCORES * 128, XW], F32) for l in range(NL)]
    tl_in = [nc.dram_tensor(f"tlin{l}", [128, 12], F32) for l in range(NL)]
    tl_out = [nc.dram_tensor(f"tlout{l}", [NCORES * 128, 12], F32) for l in range(NL)]

    with ExitStack() as es:
        fw = FW(nc, es)

        def sb(name, shape, dt=F32):
            return es.enter_context(nc.sbuf_tensor(name, shape, dt))

        xres = sb("xres", [128, KC, NT]); Bx = [[Buf(f"x{k}_{j}") for j in range(NTILE)] for k in range(KC)]
        hg = sb("hg", [128, 4, NT], BF16); Bhg = [[Buf("hg") for j in range(NTILE)] for c in range(4)]
        Ag = sb("Ag", [128, 4, NT], BF16); BAg = [[Buf("Ag") for j in range(NTILE)] for c in range(4)]
        Sloc = sb("Sloc", [128, 2, NCH, 128], BF16); BSloc = [[Buf("Sloc") for c in range(NCH)] for f in range(2)]
        qb = sb("qb", [128, 2, NT], BF16); Bqb = [[Buf("qb") for j in range(NTILE)] for f in range(2)]
        sg = sb("sg", [128, 4, NT], BF16); Bsg = [[Buf("sg") for j in range(NTILE)] for c in range(4)]
        wsl = sb("wsl", [128, NSLOT, SLOT_ELEMS], BF16); Bw = [Buf(f"w{i}") for i in range(NSLOT)]
        hb = sb("hb", [128, KC, TT], BF16); Bhb = [Buf(f"hb{k}") for k in range(KC)]
        hid = sb("hid", [128, 8, TT], BF16); Bhid = [Buf(f"hid{k}") for k in range(8)]
        xpre = sb("xpre", [128, 4, TT + 3]); Bxpre = [Buf(f"xpre{c}") for c in range(4)]
        NTMP = 9
        tmp = sb("tmp", [128, NTMP, TT]); Bt = [Buf(f"t{i}") for i in range(NTMP)]
        zaug = tmp[0:32, 7, :]; Bzaug = Bt[7]
        spt = tmp[:, 0, 0:256]; Bsp = Bt[0]
        dect = tmp[:, 1, 0:256]; Bdec = Bt[1]
        kdec = sb("kdec", [128, 256], BF16); Bkdec = Buf("kdec")
        Sst = sb("Sst", [128, 2, 256]); BS = [Buf("S0"), Buf("S1")]
        dcy = sb("dcy", [128, 2, NCH]); Bdcy = Buf("dcy")
        Dcum = sb("Dcum", [128, 2, NCH]); BDcum = Buf("Dcum")
        pvec = sb("pvec_s", [128, NL * NPV]); Bpv = Buf("pvec")
        fnorm = sb("fnorm_s", [128, KC])
        bdb = sb("bdb", [128, 8, 128], BF16); Bbd = Buf("bdb")
        waug = sb("waug_s", [17, 256]); Bwaug = Buf("waug")
        Um = sb("Um", [128, 128]); Em = sb("Em", [128, 2]); masks = sb("masks_s", [128, 24]); Bconst = Buf("const")
        ones_b = sb("ones_b", [128, 128], BF16); Bones = Buf("ones")
        epsc = sb("epsc", [128, 1]); zcol = sb("zcol", [128, 1])
        xhs = sb("xhs", [128, KC, 3]); Bxh = Buf("xhs")
        clam = sb("clam", [128, 8]); Bclam = Buf("clam")
        carry = sb("carry", [128, 8]); Bcarry = Buf("carry")
        xch = sb("xch", [128, XW]); Bxch = Buf("xch")
        tails = sb("tails", [128, 12]); Btails = Buf("tails")
        tgath = sb("tgath", [128, NCORES, 12]); Btg = Buf("tgath")
        init = sb("init", [128, 260]); Binit = Buf("init")
        Sinb = sb("Sinb", [128, 2, 128], BF16); BSinb = Buf("Sinb")
        cmb = sb("cmb", [128, 272]); Bcmb = Buf("cmb")
        banks = [es.enter_context(nc.psum_tensor(f"ps{i}", [128, 512], F32)) for i in range(8)]
        Bbank = [Buf(f"bank{i}", psum=True) for i in range(8)]
        Bg0a = Bbank[4]
        rot_state = {"i": 0, "lst": [0, 1, 2, 3]}

        def rot():
            lst = rot_state["lst"]
            i = lst[rot_state["i"] % len(lst)]
            rot_state["i"] += 1
            return banks[i], Bbank[i]
        G0, G1, G2, GS = banks[4], banks[5], banks[6], banks[7]
        Bg1, Bg2, Bgs = Bbank[5], Bbank[6], Bbank[7]

        s_x = fw.new_sem("s_x"); s_c = fw.new_sem("s_c"); s_out = fw.new_sem("s_out")
        s_w = [fw.new_sem(f"s_w{i}") for i in range(NSLOT)]
        s_misc = fw.new_sem("s_misc"); s_bd = fw.new_sem("s_bd"); s_wa = fw.new_sem("s_wa")
        cnt = {"c": 0, "misc": 0, "w": [0] * NSLOT}

        def dma_c(eng, out, in_, reads=(), writes=()):
            cnt["c"] += 1
            return fw.dma(eng, lambda h: h.dma_start(out=out, in_=in_), s_c, 16 * cnt["c"], reads=reads, writes=writes)

        def dma_m(eng, out, in_, reads=(), writes=()):
            cnt["misc"] += 1
            return fw.dma(eng, lambda h: h.dma_start(out=out, in_=in_), s_misc, 16 * cnt["misc"], reads=reads, writes=writes)

        def act(out, in_, func, reads, writes, bias=None, scale=None):
            kw = {}
            if bias is not None:
                kw["bias"] = bias
            if scale is not None:
                kw["scale"] = scale
            return fw.op("act", lambda h: h.activation(out=out, in_=in_, func=func, **kw), reads=reads, writes=writes)

        def tt(out, in0, in1, op, reads, writes, eng="dve"):
            return fw.op(eng, lambda h: h.tensor_tensor(out=out, in0=in0, in1=in1, op=op), reads=reads, writes=writes)

        def ts(out, in0, s1, s2, op0, op1, reads, writes, eng="dve"):
            if s2 is None:
                return fw.op(eng, lambda h: h.tensor_scalar(out=out, in0=in0, scalar1=s1, scalar2=None, op0=op0), reads=reads, writes=writes)
            return fw.op(eng, lambda h: h.tensor_scalar(out=out, in0=in0, scalar1=s1, scalar2=s2, op0=op0, op1=op1), reads=reads, writes=writes)

        def stt(out, in0, scalar, in1, op0, op1, reads, writes, eng="dve"):
            return fw.op(eng, lambda h: h.scalar_tensor_tensor(out=out, in0=in0, scalar=scalar, in1=in1, op0=op0, op1=op1), reads=reads, writes=writes)

        def mm(out, lhsT, rhs, start, stop, reads, writes, signal):
            return fw.op("pe", lambda h: h.matmul(out, lhsT=lhsT, rhs=rhs, start=start, stop=stop), reads=reads, writes=writes, signal=signal)

        wq = []
        wstate = {"issued": 0, "released": 0, "used": 0}

        def wview(w3, l, c0, ncols, k0=0, nk=KC):
            return w3[l, k0 * 128:(k0 + nk) * 128, c0:c0 + ncols].rearrange("(k p) m -> p k m", p=128)

        def w_issue():
            while wstate["issued"] < len(wq) and wstate["issued"] < wstate["released"] + NSLOT:
                j = wstate["issued"]
                ap, nk, ncols, kid, first = wq[j]
                slot = j % NSLOT
                n = nk * ncols
                if first or not USE_SCRATCH:
                    dst = wsl[:, slot, 0:n].rearrange("p (k m) -> p k m", m=ncols)
                    half = max(1, nk // 2)
                    k = 0
                    while k < nk:
                        kk = min(half, nk - k)
                        cnt["w"][slot] += 1
                        fw.dma("pool", lambda h, d=dst[:, k:k + kk, :], s=ap[:, k:k + kk, :]: h.dma_start(out=d, in_=s),
                               s_w[slot], 16 * cnt["w"][slot], writes=[Bw[slot]])
                        k += kk
                    Bw[slot].w = (s_w[slot], 16 * cnt["w"][slot])
                    if USE_SCRATCH and NTILE > 1:
                        wocnt[slot] += 1
                        fw.dma("sp", lambda h, slot=slot, n=n, kid=kid: h.dma_start(out=wsc.ap()[kid, :, 0:n], in_=wsl[:, slot, 0:n]),
                               s_wo[slot], 16 * wocnt[slot], reads=[Bw[slot]], writes=[Bsc[kid]])
                else:
                    cnt["w"][slot] += 1
                    fw.dma("pool", lambda h, slot=slot, n=n, kid=kid: h.dma_start(out=wsl[:, slot, 0:n], in_=wsc.ap()[kid, :, 0:n]),
                           s_w[slot], 16 * cnt["w"][slot], reads=[Bsc[kid]], writes=[Bw[slot]])
                wstate["issued"] += 1

        def wnext():
            i = wstate["used"]
            wstate["used"] += 1
            w_issue()
            assert i < wstate["issued"], "weight block not issued (too many held)"
            ap, nk, ncols, _kid, _first = wq[i]
            slot = i % NSLOT
            return wsl[:, slot, 0:nk * ncols].rearrange("p (k m) -> p k m", m=ncols), Bw[slot]

        def wrel(n=1):
            wstate["released"] += n
            assert wstate["released"] <= wstate["used"]
            w_issue()

        FFN_PARTS = [(0, 8), (8, 8), (16, 6)]
        keyid = {}

        def wadd(name, w3, l, c0, ncols, k0=0, nk=KC):
            key = (name, l, c0, k0, nk)
            first = key not in keyid
            if first:
                keyid[key] = len(keyid)
            wq.append((wview(w3, l, c0, ncols, k0, nk), nk, ncols, keyid[key], first))

        for l in range(NL):
            wadd("in", w_in, l, 0, 256)
            wadd("in", w_in, l, 256, 256)
            for j in range(NTILE):
                for b_ in range(4, 10):
                    wadd("in", w_in, l, b_ * 256, 256)
                wadd("in", w_in, l, 2560, 16)
                for b_ in range(4):
                    wadd("in", w_in, l, b_ * 256, 256)
            for j in range(NTILE):
                for b_ in range(4):
                    wadd("out", w_out, l, b_ * 256, 256)
                for (m0, nm) in FFN_PARTS:
                    for mp in range(nm // 2):
                        wadd("fi", w_fi, l, (m0 + 2 * mp) * 128, 256)
                        wadd("fi", w_fi, l, DFF + (m0 + 2 * mp) * 128, 256)
                    for b_ in range(4):
                        wadd("fo", w_fo, l, b_ * 256, 256, k0=m0, nk=nm)
        NKEY = len(keyid)
        wsc = nc.dram_tensor("wsc", [NKEY, 128, SLOT_ELEMS], BF16)
        Bsc = [Buf(f"sc{i}") for i in range(NKEY)]
        s_wo = [fw.new_sem(f"s_wo{i}") for i in range(NSLOT)]
        wocnt = [0] * NSLOT

        dma_c("sp", pvec[:], pvec_d, writes=[Bpv])
        dma_c("sp", fnorm[:], fnorm_d, writes=[Bpv])
        dma_c("sp", Um[:], U_d, writes=[Bconst])
        dma_c("sp", Em[:], E_d, writes=[Bconst])
        dma_c("sp", masks[:], masks_d, writes=[Bconst])
        dma_c("sp", xhs[:], xh_d, writes=[Bxh])
        Bpv.w = (s_c, 16 * cnt["c"]); Bconst.w = (s_c, 16 * cnt["c"]); Bxh.w = (s_c, 16 * cnt["c"])
        for k in range(KC):
            fw.dma("sp", lambda h, k=k: h.dma_start(out=xres[:, k, :], in_=xT[:, k, :]), s_x, 16 * (k + 1),
                   writes=Bx[k])
        for k in range(KC):
            for b in Bx[k]:
                b.w = (s_x, 16 * KC)
        fw.op("dve", lambda h: h.memset(ones_b[:], 1.0), writes=[Bones])
        fw.op("dve", lambda h: h.memset(epsc[:], EPS), writes=[Bconst])
        fw.op("dve", lambda h: h.memset(zcol[:], 0.0), writes=[Bconst])

        def pv(l, col, n=1):
            return pvec[:, l * NPV + col:l * NPV + col + n]

        def rmsnorm(gain_ap_fn, j, t0, n, halo=False):
            src = (lambda k: xhs[:, k, 0:n]) if halo else (lambda k: xres[:, k, t0:t0 + n])
            Bsrc = (lambda k: Bxh) if halo else (lambda k: Bx[k][j])
            for k in range(KC):
                act(hid[:, k, 0:n], src(k), AF.Square, reads=[Bsrc(k)], writes=[Bhid[k]])
            bank, Bb = rot()
            for k in range(KC):
                mm(bank[:, 0:n], ones_b[:], hid[:, k, 0:n], k == 0, k == KC - 1, reads=[Bones, Bhid[k]], writes=[Bb], signal=(k == KC - 1))
            rs = tmp[:, 8, 0:n]
            act(rs, bank[:, 0:n], AF.Ln, reads=[Bb, Bconst], writes=[Bt[8]], bias=epsc[:], scale=1.0 / D)
            act(rs, rs, AF.Exp, reads=[Bt[8]], writes=[Bt[8]], scale=-0.5)
            for k in range(KC):
                stt(hb[:, k, 0:n], src(k), gain_ap_fn(k), rs, ALU.mult, ALU.mult,
                    reads=[Bsrc(k), Bt[8], Bpv], writes=[Bhb[k]])

        cc_sems = []
        for l in range(NL):
            for g in range(8):
                fw.dma("pool", lambda h, g=g, l=l: h.dma_start(out=bdb[:, g, :], in_=bd_d[:, l, g, :]), s_bd, 16 * (8 * l + g + 1), writes=[Bbd])
            Bbd.w = (s_bd, 16 * 8 * (l + 1))
            fw.dma("sp", lambda h, l=l: h.dma_start(out=waug[:], in_=waug_d[:, l, :]), s_wa, 16 * (l + 1), writes=[Bwaug])
            act(clam[:, 0:4], pv(l, PV_LAM, 4), AF.Exp, reads=[Bpv], writes=[Bclam], scale=-1.0)
            act(clam[:, 0:4], clam[:, 0:4], AF.Ln, reads=[Bclam], writes=[Bclam], bias=1.0)
            ts(clam[:, 4:8], clam[:, 0:4], -16.0, None, ALU.mult, None, reads=[Bclam], writes=[Bclam])
            ts(clam[:, 0:4], clam[:, 0:4], -8.0, None, ALU.mult, None, reads=[Bclam], writes=[Bclam])

            rmsnorm(lambda k: pv(l, PV_G1 + k), NTILE - 1, NT - 3, 3, halo=(l == 0))
            for blk in range(2):
                wv, Bwv = wnext()
                for c2 in range(2):
                    ct = blk * 2 + c2
                    for k in range(KC):
                        mm(GS[:, 16 + ct * 3:16 + ct * 3 + 3], wv[:, k, c2 * 128:(c2 + 1) * 128], hb[:, k, 0:3], k == 0, k == KC - 1,
                           reads=[Bwv, Bhb[k]], writes=[Bgs], signal=(k == KC - 1))
                wrel()
            fw.op("dve", lambda h: h.tensor_copy(out=tails[:], in_=GS[:, 16:28]), reads=[Bgs], writes=[Btails])
            if l > 0:
                cnt["misc"] += 1
                Btl_in, Btl_out = Buf("tlin"), Buf("tlout")
                fw.dma("pool", lambda h, l=l: h.dma_start(out=tl_in[l].ap(), in_=tails[:]), s_misc, 16 * cnt["misc"], reads=[Btails], writes=[Btl_in])
                cs = fw.new_sem(f"cc_t{l}")
                fw.dma("pool", lambda h, l=l: h.collective_compute("AllGather", ALU.bypass, replica_groups=[list(range(NCORES))],
                                                                     ins=[tl_in[l].ap().opt()], outs=[tl_out[l].ap().opt()]),
                       cs, 1, reads=[Btl_in], writes=[Btl_out], inc=None)
                cnt["misc"] += 1
                fw.dma("pool", lambda h, l=l: h.dma_start(out=tgath[:], in_=tl_out[l].ap().rearrange("(r p) w -> p r w", p=128)),
                       s_misc, 16 * cnt["misc"], reads=[Btl_out], writes=[Btg])
            else:
                fw.op("dve", lambda h: h.tensor_copy(out=xpre[:, :, 0:3], in_=tails[:].rearrange("p (c t) -> p c t", t=3)), reads=[Btails], writes=Bxpre)
            def halo_sum():
                xh = xpre[:, :, 0:3]
                ts(tails[:], tgath[:, 0, :], masks[:, 8:9], None, ALU.mult, None, reads=[Btg, Bconst], writes=[Btails])
                for r in range(1, NCORES):
                    stt(tails[:], tgath[:, r, :], masks[:, 8 + r:9 + r], tails[:], ALU.mult, ALU.add, reads=[Btg, Bconst, Btails], writes=[Btails])
                fw.op("dve", lambda h: h.tensor_copy(out=xh, in_=tails[:].rearrange("p (c t) -> p c t", t=3)), reads=[Btails], writes=Bxpre)

            rot_state["lst"] = [0, 1, 2, 3]
            for j in range(NTILE):
                t0 = j * TT
                rmsnorm(lambda k: pv(l, PV_G1 + k), j, t0, TT)
                wv, Bwv = wnext()
                for ft in range(2):
                    bank, Bb = rot()
                    for k in range(KC):
                        mm(bank[:], wv[:, k, ft * 128:(ft + 1) * 128], hb[:, k, :], k == 0, k == KC - 1,
                           reads=[Bwv, Bhb[k]], writes=[Bb], signal=(k == KC - 1))
                    act(qb[:, ft, t0:t0 + TT], bank[:], AF.Copy, reads=[Bb], writes=[Bqb[ft][j]], scale=0.125)
                wrel()
                wk, Bwk = wnext()
                wv0, Bwv0 = wnext()
                wv1, Bwv1 = wnext()
                kraw = [tmp[:, 3 + s // 2, (s % 2) * 256:(s % 2) * 256 + 256] for s in range(4)]
                Bkraw = [Bt[3 + s // 2] for s in range(4)]
                vbs = [tmp[:, 5 + s // 2, :].bitcast(BF16)[:, (s % 2) * 512:(s % 2) * 512 + 512] for s in range(4)]
                Bvbs = [Bt[5 + s // 2] for s in range(4)]
                for s in range(4):
                    for k in range(KC):
                        mm(G0[:, 0:256], hb[:, k, s * 128:(s + 1) * 128], wk[:, k, :], k == 0, k == KC - 1,
                           reads=[Bwk, Bhb[k]], writes=[Bg0a], signal=(k == KC - 1))
                    fw.op("dve", lambda h, s=s: h.tensor_copy(out=kraw[s], in_=G0[:, 0:256]), reads=[Bg0a], writes=[Bkraw[s]])
                    for k in range(KC):
                        mm(G1[:, 0:256], hb[:, k, s * 128:(s + 1) * 128], wv0[:, k, :], k == 0, k == KC - 1,
                           reads=[Bwv0, Bhb[k]], writes=[Bg1], signal=False)
                    for k in range(KC):
                        mm(G1[:, 256:512], hb[:, k, s * 128:(s + 1) * 128], wv1[:, k, :], k == 0, k == KC - 1,
                           reads=[Bwv1, Bhb[k]], writes=[Bg1], signal=(k == KC - 1))
                    act(vbs[s], G1[:], AF.Copy, reads=[Bg1], writes=[Bvbs[s]])
                wrel(3)
                for gb in range(2):
                    wv, Bwv = wnext()
                    for c2 in range(2):
                        ct = gb * 2 + c2
                        bank, Bb = rot()
                        for k in range(KC):
                            mm(bank[:], wv[:, k, c2 * 128:(c2 + 1) * 128], hb[:, k, :], k == 0, k == KC - 1,
                               reads=[Bwv, Bhb[k]], writes=[Bb], signal=(k == KC - 1))
                        act(sg[:, ct, t0:t0 + TT], bank[:], AF.Silu, reads=[Bb], writes=[Bsg[ct][j]])
                    wrel()
                wv, Bwv = wnext()
                bank, Bb = rot()
                for k in range(KC):
                    mm(bank[0:16, :], wv[:, k, 0:16], hb[:, k, :], k == 0, k == KC - 1,
                       reads=[Bwv, Bhb[k]], writes=[Bb], signal=(k == KC - 1))
                wrel()
                fw.op("dve", lambda h: h.memset(zaug, 1.0), writes=[Bzaug])
                fw.op("dve", lambda h, bank=bank: h.tensor_copy(out=zaug[0:16, :], in_=bank[0:16, :]), reads=[Bb], writes=[Bzaug])
                for s in range(4):
                    mm(GS[:, 256:512], zaug[0:17, s * 128:(s + 1) * 128], waug[:, :], True, True, reads=[Bzaug, Bwaug], writes=[Bgs], signal=True)
                    act(spt, GS[:, 256:512], AF.Exp, reads=[Bgs], writes=[Bsp], scale=-1.0)
                    act(spt, spt, AF.Ln, reads=[Bsp], writes=[Bsp], bias=1.0)
                    mm(GS[:, 256:512], Um[:], spt, True, True, reads=[Bconst, Bsp], writes=[Bgs], signal=True)
                    act(dect, GS[:, 256:512], AF.Exp, reads=[Bgs], writes=[Bdec], scale=-1.0 / 16.0)
                    tt(kdec[:], kraw[s], dect, ALU.mult, reads=[Bkraw[s], Bdec], writes=[Bkdec])
                    for ft in range(2):
                        mm(GS[:, ft * 2:ft * 2 + 2], spt[:, ft * 128:(ft + 1) * 128], Em[:], True, True, reads=[Bsp, Bconst], writes=[Bgs], signal=(ft == 1))
                    c0 = j * 8 + s * 2
                    act(dcy[:, :, c0:c0 + 2], GS[:, 0:4].rearrange("p (f c) -> p f c", c=2), AF.Exp, reads=[Bgs], writes=[Bdcy], scale=-1.0 / 16.0)
                    for cc in range(2):
                        c = c0 + cc
                        for ft in range(2):
                            mm(G2[:, ft * 256:(ft + 1) * 256], kdec[cc * 64:(cc + 1) * 64, ft * 128:(ft + 1) * 128],
                               vbs[s][cc * 64:(cc + 1) * 64, ft * 256:(ft + 1) * 256], True, True,
                               reads=[Bkdec, Bvbs[s]], writes=[Bg2], signal=(ft == 1))
                        for ft in range(2):
                            if c == 0:
                                fw.op("dve", lambda h, ft=ft: h.tensor_copy(out=Sst[:, ft, :], in_=G2[:, ft * 256:(ft + 1) * 256]), reads=[Bg2], writes=[BS[ft]])
                            else:
                                stt(Sst[:, ft, :], Sst[:, ft, :], dcy[:, ft, c:c + 1], G2[:, ft * 256:(ft + 1) * 256], ALU.mult, ALU.add,
                                    reads=[BS[ft], Bdcy, Bg2], writes=[BS[ft]])
                            for h2 in range(2):
                                act(Sloc[h2 * 64:(h2 + 1) * 64, ft, c, :], Sst[h2 * 64:(h2 + 1) * 64, ft, h2 * 128:(h2 + 1) * 128], AF.Copy,
                                    reads=[BS[ft]], writes=[BSloc[ft][c]])

                if j == 0 and l > 0:
                    halo_sum()
                wblocks = [wnext() for _ in range(2)]
                px = []
                for ct in range(4):
                    wv, Bwv = wblocks[ct // 2]
                    bank, Bb = rot()
                    for k in range(KC):
                        mm(bank[:], wv[:, k, (ct % 2) * 128:(ct % 2 + 1) * 128], hb[:, k, :], k == 0, k == KC - 1,
                           reads=[Bwv, Bhb[k]], writes=[Bb], signal=(k == KC - 1))
                    act(xpre[:, ct, 3:3 + TT], bank[:], AF.Copy, reads=[Bb], writes=[Bxpre[ct]])
                wrel(2)
                wblocks = [wnext() for _ in range(2)]
                for ct in range(4):
                    xc, Bxc = tmp[:, 0, :], Bt[0]
                    cw = lambda jj: pv(l, PV_CW + jj * 4 + ct)
                    ts(xc, xpre[:, ct, 0:TT], cw(0), pv(l, PV_CB + ct), ALU.mult, ALU.add, reads=[Bxpre[ct], Bpv], writes=[Bxc])
                    for jj in range(1, 4):
                        stt(xc, xpre[:, ct, jj:jj + TT], cw(jj), xc, ALU.mult, ALU.add, reads=[Bxpre[ct], Bpv, Bxc], writes=[Bxc])
                    fw.op("dve", lambda h, ct=ct: h.tensor_copy(out=xpre[:, ct, 0:3], in_=xpre[:, ct, TT:TT + 3]), reads=[Bxpre[ct]], writes=[Bxpre[ct]])
                    xcb = tmp[:, 1, :].bitcast(BF16)[:, 0:TT]
                    Bxcb = Bt[1]
                    act(xcb, xc, AF.Copy, reads=[Bxc], writes=[Bxcb])
                    bank_r, Bbr = rot()
                    mm(bank_r[:], bdb[:, ct, :], xcb, True, True, reads=[Bbd, Bxcb], writes=[Bbr], signal=True)
                    bank_i, Bbi = rot()
                    mm(bank_i[:], bdb[:, 4 + ct, :], xcb, True, True, reads=[Bbd, Bxcb], writes=[Bbi], signal=True)
                    r_t, Br = tmp[:, 2, :], Bt[2]
                    i_t, Bi = tmp[:, 3, :], Bt[3]
                    a_t, Ba = tmp[:, 4, :], Bt[4]
                    s_t, Bs = tmp[:, 5, :], Bt[5]
                    hl, Bhl = tmp[:, 6, :], Bt[6]
                    Ac, BAc = tmp[:, 7, :], Bt[7]
                    act(r_t, bank_r[:], AF.Sigmoid, reads=[Bbr, Bpv], writes=[Br], bias=pv(l, PV_BA + ct))
                    act(i_t, bank_i[:], AF.Sigmoid, reads=[Bbi, Bpv], writes=[Bi], bias=pv(l, PV_BI + ct))
                    act(a_t, r_t, AF.Exp, reads=[Br, Bclam], writes=[Ba], scale=clam[:, ct:ct + 1])
                    act(s_t, r_t, AF.Exp, reads=[Br, Bclam], writes=[Bs], scale=clam[:, 4 + ct:5 + ct])
                    act(s_t, s_t, AF.Sqrt, reads=[Bs], writes=[Bs], scale=-1.0, bias=1.0)
                    tt(s_t, s_t, i_t, ALU.mult, reads=[Bs, Bi], writes=[Bs])
                    tt(s_t, s_t, xc, ALU.mult, reads=[Bs, Bxc], writes=[Bs])
                    h0 = 0.0 if j == 0 else carry[:, ct:ct + 1]
                    A0 = 1.0 if j == 0 else carry[:, 4 + ct:5 + ct]
                    fw.op("dve", lambda h, h0=h0, hl=hl, a_t=a_t, s_t=s_t: h.tensor_tensor_scan(out=hl, data0=a_t, data1=s_t, initial=h0, op0=ALU.mult, op1=ALU.add),
                          reads=[Ba, Bs, Bcarry], writes=[Bhl])
                    fw.op("dve", lambda h, A0=A0, Ac=Ac, a_t=a_t: h.tensor_tensor_scan(out=Ac, data0=a_t, data1=zcol[:, 0:1].to_broadcast([128, TT]), initial=A0, op0=ALU.mult, op1=ALU.add),
                          reads=[Ba, Bcarry, Bconst], writes=[BAc])
                    fw.op("dve", lambda h, ct=ct, hl=hl: h.tensor_copy(out=carry[:, ct:ct + 1], in_=hl[:, TT - 1:TT]), reads=[Bhl], writes=[Bcarry])
                    fw.op("dve", lambda h, ct=ct, Ac=Ac: h.tensor_copy(out=carry[:, 4 + ct:5 + ct], in_=Ac[:, TT - 1:TT]), reads=[BAc], writes=[Bcarry])
                    wv, Bwv = wblocks[ct // 2]
                    bank, Bb = rot()
                    for k in range(KC):
                        mm(bank[:], wv[:, k, (ct % 2) * 128:(ct % 2 + 1) * 128], hb[:, k, :], k == 0, k == KC - 1,
                           reads=[Bwv, Bhb[k]], writes=[Bb], signal=(k == KC - 1))
                    gl, Bgl = tmp[:, 2, :], Bt[2]
                    act(gl, bank[:], AF.Gelu_apprx_tanh, reads=[Bb], writes=[Bgl])
                    tt(hg[:, ct, t0:t0 + TT], hl, gl, ALU.mult, reads=[Bhl, Bgl], writes=[Bhg[ct][j]])
                    tt(Ag[:, ct, t0:t0 + TT], Ac, gl, ALU.mult, reads=[BAc, Bgl], writes=[BAg[ct][j]])
                wrel(2)

            for ft in range(2):
                fw.op("dve", lambda h, ft=ft: h.tensor_tensor_scan(out=Dcum[:, ft, :], data0=dcy[:, ft, :], data1=zcol[:, 0:1].to_broadcast([128, NCH]),
                                                                    initial=1.0, op0=ALU.mult, op1=ALU.add), reads=[Bdcy, Bconst], writes=[BDcum])
            fw.op("dve", lambda h: h.tensor_copy(out=xch[:, 0:8], in_=carry[:, 0:8]), reads=[Bcarry], writes=[Bxch])
            for ft in range(2):
                for h2 in range(2):
                    fw.op("dve", lambda h, ft=ft, h2=h2: h.tensor_copy(out=xch[h2 * 64:(h2 + 1) * 64, 8 + ft * 128:8 + (ft + 1) * 128],
                                                                         in_=Sst[h2 * 64:(h2 + 1) * 64, ft, h2 * 128:(h2 + 1) * 128]),
                          reads=[BS[ft]], writes=[Bxch])
            fw.op("dve", lambda h: h.tensor_copy(out=xch[:, 264:266], in_=Dcum[:, :, NCH - 1]), reads=[BDcum], writes=[Bxch])
            Bag_in, Bag_out = Buf("agin"), Buf("agout")
            cnt["misc"] += 1
            fw.dma("pool", lambda h, l=l: h.dma_start(out=ag_in[l].ap(), in_=xch[:]), s_misc, 16 * cnt["misc"], reads=[Bxch], writes=[Bag_in])
            cs = fw.new_sem(f"cc_s{l}")
            fw.dma("pool", lambda h, l=l: h.collective_compute("AllGather", ALU.bypass, replica_groups=[list(range(NCORES))],
                                                                 ins=[ag_in[l].ap().opt()], outs=[ag_out[l].ap().opt()]),
                   cs, 1, reads=[Bag_in], writes=[Bag_out], inc=None)
            gath = tmp[:, 0:5, :].rearrange("p a b -> p (a b)")[:, 0:NCORES * XW].rearrange("p (r w) -> p r w", w=XW)
            Bgath = Bt[0:5]
            cnt["misc"] += 1
            fw.dma("pool", lambda h, l=l: h.dma_start(out=gath, in_=ag_out[l].ap().rearrange("(r p) w -> p r w", p=128)),
                   s_misc, 16 * cnt["misc"], reads=[Bag_out], writes=Bgath)
            mB = lambda n: masks[:, 0:8].unsqueeze(2).to_broadcast([128, NCORES, n])
            omB = lambda n: masks[:, 16:24].unsqueeze(2).to_broadcast([128, NCORES, n])
            RG = Bgath + [Bconst]
            tt(gath[:, :, 0:4], gath[:, :, 0:4], mB(4), ALU.mult, reads=RG, writes=Bgath)
            tt(gath[:, :, 8:264], gath[:, :, 8:264], mB(256), ALU.mult, reads=RG, writes=Bgath)
            tt(gath[:, :, 4:8], gath[:, :, 4:8], mB(4), ALU.mult, reads=RG, writes=Bgath)
            tt(gath[:, :, 4:8], gath[:, :, 4:8], omB(4), ALU.add, reads=RG, writes=Bgath)
            tt(gath[:, :, 264:266], gath[:, :, 264:266], mB(2), ALU.mult, reads=RG, writes=Bgath)
            tt(gath[:, :, 264:266], gath[:, :, 264:266], omB(2), ALU.add, reads=RG, writes=Bgath)
            for c4 in range(4):
                fw.op("dve", lambda h, c4=c4: h.tensor_tensor_scan(out=cmb[:, c4 * 8:(c4 + 1) * 8], data0=gath[:, :, 4 + c4], data1=gath[:, :, c4],
                                                                    initial=0.0, op0=ALU.mult, op1=ALU.add), reads=Bgath, writes=[Bcmb])
            fw.op("dve", lambda h: h.tensor_copy(out=init[:, 0:4], in_=cmb[:, 0:32].rearrange("p (c r) -> p c r", r=8)[:, :, 7]), reads=[Bcmb], writes=[Binit])
            for ft in range(2):
                sl = slice(4 + ft * 128, 4 + (ft + 1) * 128)
                fw.op("dve", lambda h, sl=sl, ft=ft: h.tensor_copy(out=init[:, sl], in_=gath[:, 0, 8 + ft * 128:8 + (ft + 1) * 128]), reads=Bgath, writes=[Binit])
                for r in range(1, NCORES):
                    stt(init[:, sl], init[:, sl], gath[:, r, 264 + ft:265 + ft], gath[:, r, 8 + ft * 128:8 + (ft + 1) * 128], ALU.mult, ALU.add,
                        reads=Bgath + [Binit], writes=[Binit])
            fw.op("dve", lambda h: h.tensor_copy(out=Sinb[:], in_=init[:, 4:260].rearrange("p (f v) -> p f v", v=128)), reads=[Binit], writes=[BSinb])

            rot_state["lst"] = [0, 1, 2, 3, 5, 6]
            for j in range(NTILE):
                t0 = j * TT
                for ct in range(4):
                    stt(hb[:, ct, :], Ag[:, ct, t0:t0 + TT], init[:, ct:ct + 1], hg[:, ct, t0:t0 + TT], ALU.mult, ALU.add,
                        reads=[BAg[ct][j], Bhg[ct][j], Binit], writes=[Bhb[ct]])
                qD = tmp[:, 0, :].bitcast(BF16)
                BqD = Bt[0]
                for ft in range(2):
                    tt(qD[:, ft * TT:(ft + 1) * TT].rearrange("p (c k) -> p c k", k=64),
                       qb[:, ft, t0:t0 + TT].rearrange("p (c k) -> p c k", k=64),
                       Dcum[:, ft, j * 8:(j + 1) * 8].unsqueeze(2).to_broadcast([128, 8, 64]), ALU.mult,
                       reads=[Bqb[ft][j], BDcum], writes=[BqD])
                for hd in range(4):
                    ft, h2 = hd // 2, hd % 2
                    pr = slice(h2 * 64, (h2 + 1) * 64)
                    bank, Bb = rot()
                    for cc in range(8):
                        c = j * 8 + cc
                        mm(bank[:, cc * 64:(cc + 1) * 64], Sloc[pr, ft, c, :], qb[pr, ft, t0 + cc * 64:t0 + (cc + 1) * 64], True, False,
                           reads=[BSloc[ft][c], Bqb[ft][j]], writes=[Bb], signal=False)
                        mm(bank[:, cc * 64:(cc + 1) * 64], Sinb[pr, ft, :], qD[pr, ft * TT + cc * 64:ft * TT + (cc + 1) * 64], False, True,
                           reads=[BSinb, BqD], writes=[Bb], signal=(cc == 7))
                    osq = tmp[:, 1, :].bitcast(BF16)[:, 0:TT]
                    act(osq, bank[:], AF.Square, reads=[Bb], writes=[Bt[1]])
                    bank2, Bb2 = rot()
                    mm(bank2[:], ones_b[:], osq, True, True, reads=[Bones, Bt[1]], writes=[Bb2], signal=True)
                    rs = tmp[:, 2, :]
                    act(rs, bank2[:], AF.Ln, reads=[Bb2, Bconst], writes=[Bt[2]], bias=epsc[:], scale=1.0 / 128.0)
                    act(rs, rs, AF.Exp, reads=[Bt[2]], writes=[Bt[2]], scale=-0.5)
                    y = tmp[:, 3, :]
                    stt(y, bank[:], pv(l, PV_GN), rs, ALU.mult, ALU.mult, reads=[Bb, Bt[2], Bpv], writes=[Bt[3]])
                    tt(hb[:, 4 + hd, :], y, sg[:, hd, t0:t0 + TT], ALU.mult, reads=[Bt[3], Bsg[hd][j]], writes=[Bhb[4 + hd]])
                for b in range(4):
                    wv, Bwv = wnext()
                    for c2 in range(2):
                        m = b * 2 + c2
                        bank, Bb = rot()
                        for k in range(KC):
                            mm(bank[:], wv[:, k, c2 * 128:(c2 + 1) * 128], hb[:, k, :], k == 0, k == KC - 1,
                               reads=[Bwv, Bhb[k]], writes=[Bb], signal=(k == KC - 1))
                        tt(xres[:, m, t0:t0 + TT], xres[:, m, t0:t0 + TT], bank[:], ALU.add, reads=[Bx[m][j], Bb], writes=[Bx[m][j]])
                    wrel()
                rmsnorm(lambda k: pv(l, PV_G2 + k), j, t0, TT)
                for (m0, nm) in FFN_PARTS:
                    for mp in range(nm // 2):
                        wg, Bwg = wnext()
                        gbanks = []
                        for c2 in range(2):
                            bg, Bbg = rot()
                            for k in range(KC):
                                mm(bg[:], wg[:, k, c2 * 128:(c2 + 1) * 128], hb[:, k, :], k == 0, k == KC - 1,
                                   reads=[Bwg, Bhb[k]], writes=[Bbg], signal=(k == KC - 1))
                            gbanks.append((bg, Bbg))
                        wrel()
                        wu, Bwu = wnext()
                        ubanks = []
                        for c2 in range(2):
                            bu, Bbu = rot()
                            for k in range(KC):
                                mm(bu[:], wu[:, k, c2 * 128:(c2 + 1) * 128], hb[:, k, :], k == 0, k == KC - 1,
                                   reads=[Bwu, Bhb[k]], writes=[Bbu], signal=(k == KC - 1))
                            ubanks.append((bu, Bbu))
                        wrel()
                        for c2 in range(2):
                            jj = mp * 2 + c2
                            bg, Bbg = gbanks[c2]
                            bu, Bbu = ubanks[c2]
                            sgt = tmp[:, 4 + (jj % 2), :]
                            Bsgt = Bt[4 + (jj % 2)]
                            act(sgt, bg[:], AF.Silu, reads=[Bbg], writes=[Bsgt])
                            tt(hid[:, jj, :], sgt, bu[:], ALU.mult, reads=[Bsgt, Bbu], writes=[Bhid[jj]])
                    for b in range(4):
                        wv, Bwv = wnext()
                        for c2 in range(2):
                            m = b * 2 + c2
                            bank, Bb = rot()
                            for k in range(nm):
                                mm(bank[:], wv[:, k, c2 * 128:(c2 + 1) * 128], hid[:, k, :], k == 0, k == nm - 1,
                                   reads=[Bwv, Bhid[k]], writes=[Bb], signal=(k == nm - 1))
                            tt(xres[:, m, t0:t0 + TT], xres[:, m, t0:t0 + TT], bank[:], ALU.add, reads=[Bx[m][j], Bb], writes=[Bx[m][j]])
                        wrel()

        rot_state["lst"] = [0, 1, 2, 3, 5, 6]
        evs = []
        nout = 0
        for j in range(NTILE):
            t0 = j * TT
            for k in range(KC):
                act(hid[:, k, :], xres[:, k, t0:t0 + TT], AF.Square, reads=[Bx[k][j]], writes=[Bhid[k]])
            bank, Bb = rot()
            for k in range(KC):
                mm(bank[:], ones_b[:], hid[:, k, :], k == 0, k == KC - 1, reads=[Bones, Bhid[k]], writes=[Bb], signal=(k == KC - 1))
            rs = tmp[:, 8, :]
            act(rs, bank[:], AF.Ln, reads=[Bb, Bconst], writes=[Bt[8]], bias=epsc[:], scale=1.0 / D)
            act(rs, rs, AF.Exp, reads=[Bt[8]], writes=[Bt[8]], scale=-0.5)
            for k in range(KC):
                stt(xres[:, k, t0:t0 + TT], xres[:, k, t0:t0 + TT], fnorm[:, k:k + 1], rs, ALU.mult, ALU.mult,
                    reads=[Bx[k][j], Bt[8], Bpv], writes=[Bx[k][j]])
                nout += 1
                evs.append(fw.dma("sp", lambda h, k=k, t0=t0: h.dma_start(out=yT[:, k, t0:t0 + TT], in_=xres[:, k, t0:t0 + TT]),
                                  s_out, 16 * nout, reads=[Bx[k][j]]))
        fw.wait_only("sp", [(s_out, 16 * nout)])
        assert wstate["used"] == len(wq) == wstate["released"], (wstate, len(wq))
        fw.finish()
    return nc


def host_inputs(inputs, NT, NL, ncores=NCORES):
    f = lambda a: np.ascontiguousarray(np.asarray(a, dtype=np.float32))
    x = f(inputs["x"])[0]
    S = x.shape[0]
    assert S == NT * ncores
    pvec = np.zeros((128, NL, NPV), np.float32)

    def col(v, n):
        return f(v).reshape(n, 128).T
    for l in range(NL):
        pvec[:, l, PV_G1:PV_G1 + 8] = col(inputs["norm1"][l], 8)
        pvec[:, l, PV_G2:PV_G2 + 8] = col(inputs["norm2"][l], 8)
        for jj in range(4):
            pvec[:, l, PV_CW + jj * 4:PV_CW + jj * 4 + 4] = col(inputs["conv_w"][l][jj], 4)
        pvec[:, l, PV_CB:PV_CB + 4] = col(inputs["conv_b"][l], 4)
        pvec[:, l, PV_BA:PV_BA + 4] = col(inputs["lru_ba"][l], 4)
        pvec[:, l, PV_BI:PV_BI + 4] = col(inputs["lru_bi"][l], 4)
        pvec[:, l, PV_LAM:PV_LAM + 4] = col(inputs["lru_lambda"][l], 4)
        pvec[:, l, PV_GN] = f(inputs["gla_norm"][l])
    pvec = pvec.reshape(128, NL * NPV)
    fnorm = col(inputs["final_norm"], 8)
    bd = np.zeros((128, NL, 8, 128), np.float32)
    wa, wi = f(inputs["lru_wa"]), f(inputs["lru_wi"])
    for l in range(NL):
        for ct in range(4):
            for h2 in range(2):
                bd[h2 * 64:(h2 + 1) * 64, l, ct, h2 * 64:(h2 + 1) * 64] = wa[l, ct * 2 + h2]
                bd[h2 * 64:(h2 + 1) * 64, l, 4 + ct, h2 * 64:(h2 + 1) * 64] = wi[l, ct * 2 + h2]
    waug = np.zeros((17, NL, 256), np.float32)
    for l in range(NL):
        waug[0:16, l] = f(inputs["gla_w_alpha"][l])
        waug[16, l] = f(inputs["gla_b_alpha"][l])
    idx = np.arange(128)
    U = ((idx[:, None] > idx[None, :]) & ((idx[:, None] // 64) == (idx[None, :] // 64))).astype(np.float32)
    E = (idx[:, None] // 64 == np.arange(2)[None, :]).astype(np.float32)
    shared = {
        "w_in": f(inputs["w_in"])[:NL], "w_out": f(inputs["w_out"])[:NL],
        "w_ffn_in": f(inputs["w_ffn_in"])[:NL], "w_ffn_out": f(inputs["w_ffn_out"])[:NL],
        "pvec": pvec, "fnorm": np.ascontiguousarray(fnorm), "bd": bd, "waug": waug, "Umat": U, "Emat": E,
    }
    in_maps = []
    for c in range(ncores):
        xs = x[c * NT:(c + 1) * NT]
        xTc = np.ascontiguousarray(xs.T.reshape(KC, 128, NT).transpose(1, 0, 2))
        masks = np.zeros((128, 24), np.float32)
        masks[:, 0:8] = (np.arange(8) < c).astype(np.float32)[None, :]
        masks[:, 16:24] = 1.0 - masks[:, 0:8]
        if c > 0:
            masks[:, 8 + c - 1] = 1.0
        m = dict(shared)
        m["xT"] = xTc
        if c > 0:
            xh = x[c * NT - 3:c * NT]
            m["xhalo"] = np.ascontiguousarray(xh.T.reshape(KC, 128, 3).transpose(1, 0, 2))
        else:
            m["xhalo"] = np.zeros((128, KC, 3), np.float32)
        m["masks"] = masks
        in_maps.append(m)
    return in_maps


def run(inputs, NT, NL, trace=False):
    nc = build(NT, NL)
    in_maps = host_inputs(inputs, NT, NL)
    res = run_bass_kernel_spmd(nc, in_maps, core_ids=list(range(NCORES)))
    outs = []
    for c in range(NCORES):
        yTc = res.results[c]["yT"]
        outs.append(yTc.transpose(2, 1, 0).reshape(NT, D))
    return np.concatenate(outs, axis=0)[None].astype(np.float32)


def kernel(**inputs):
    return run(inputs, 2048, 4)
```
